# Optimizing a Trainium2 kernel written in Bass

```python
import math
import jax, jax.numpy as jnp
from jax import lax
import numpy as np

D_MODEL = 4096
BATCH = 8
SEQ = 2048
DEPTH = 2

N_A = DEPTH // 2
N_B = DEPTH - N_A
HEAD_DIM = 128
POOL_WINDOWS = (2, 4, 8, 16)
N_POOL_GROUPS = 4
POOL_WIDTH = 3 * D_MODEL // 4
POOL_GROUP = POOL_WIDTH // N_POOL_GROUPS
MEM_LEN = 256
MEM_HEADS = 4
MEM_HEAD_DIM = D_MODEL // 16
MEM_WIDTH = MEM_HEADS * MEM_HEAD_DIM
DIL_CONFIGS = ((128, 1), (512, 4), (2048, 16))
N_DIL_GROUPS = len(DIL_CONFIGS)
DIL_HEADS = D_MODEL // 512
DIL_Q_WIDTH = N_DIL_GROUPS * DIL_HEADS * HEAD_DIM
DIL_OUT_WIDTH = DIL_HEADS * HEAD_DIM
KV_WIDTH = 2 * DIL_Q_WIDTH
A_IN_WIDTH = POOL_WIDTH + MEM_WIDTH
A_OUT_WIDTH = POOL_WIDTH + MEM_WIDTH
B_IN_WIDTH = DIL_Q_WIDTH + MEM_WIDTH
B_OUT_WIDTH = DIL_OUT_WIDTH + MEM_WIDTH
D_FF = 4 * D_MODEL
NUM_BUCKETS = 32
MAX_DISTANCE = 2048
EPS = 1e-6

kernel_name = "yoco_pool_dilated_hybrid"


def rmsnorm(x, g):
    x32 = x.astype(jnp.float32)
    y = x32 * lax.rsqrt(jnp.mean(x32 * x32, axis=-1, keepdims=True) + EPS)
    return (y * g.astype(jnp.float32)).astype(x.dtype)


def sq_relu_mlp(h, w1, w2):
    a = jax.nn.relu(h @ w1)
    return (a * a) @ w2


def t5_bucket(dist):
    max_exact = NUM_BUCKETS // 2
    d32 = jnp.maximum(dist, 1).astype(jnp.float32)
    large = max_exact + (jnp.log(d32 / max_exact) / math.log(MAX_DISTANCE / max_exact)
                         * (NUM_BUCKETS - max_exact)).astype(jnp.int32)
    large = jnp.minimum(large, NUM_BUCKETS - 1)
    return jnp.where(dist < max_exact, dist, large)


def pool_mixer(u, w_pg, scale):
    b, s, _ = u.shape
    ug = u.reshape(b, s, N_POOL_GROUPS, POOL_GROUP).astype(jnp.float32)
    cs = jnp.cumsum(ug, axis=1)
    pos = jnp.arange(s)
    outs = []
    for g, w in enumerate(POOL_WINDOWS):
        c = cs[:, :, g]
        lag = jnp.pad(c, ((0, 0), (w, 0), (0, 0)))[:, :s]
        cnt = jnp.minimum(pos + 1, w).astype(jnp.float32)[None, :, None]
        outs.append((c - lag) / cnt - ug[:, :, g])
    pooled = jnp.stack(outs, axis=2).astype(u.dtype)
    mixed = jnp.einsum('bsgc,gcd->bsgd', pooled, w_pg).reshape(b, s, POOL_WIDTH)
    return mixed * scale


def memory_attention(u_mem, mk, mv):
    b, s, _ = u_mem.shape
    q = u_mem.reshape(b, s, MEM_HEADS, MEM_HEAD_DIM)
    logits = jnp.einsum('bshc,bmhc->bhsm', q, mk).astype(jnp.float32) / math.sqrt(MEM_HEAD_DIM)
    p = jax.nn.softmax(logits, axis=-1).astype(mv.dtype)
    o = jnp.einsum('bhsm,bmhc->bshc', p, mv)
    return o.reshape(b, s, MEM_WIDTH)


def dilated_group(q, k, v, window, dil, bias_g):
    b, s, h, hd = q.shape
    wd = window // dil
    blk = wd
    L = s // dil
    nblk = -(-L // blk)
    lp = nblk * blk

    def to_blocks(t):
        t = t.reshape(b, L, dil, h, hd).transpose(0, 3, 2, 1, 4)
        t = jnp.pad(t, ((0, 0), (0, 0), (0, 0), (0, lp - L), (0, 0)))
        return t.reshape(b, h, dil, nblk, blk, hd)

    def with_prev(t):
        prev = jnp.pad(t, ((0, 0), (0, 0), (0, 0), (1, 0), (0, 0), (0, 0)))[:, :, :, :nblk]
        return jnp.concatenate([prev, t], axis=4)

    qb = to_blocks(q)
    kk = with_prev(to_blocks(k))
    vv = with_prev(to_blocks(v))

    qi = jnp.arange(blk)[:, None]
    kj = jnp.arange(2 * blk)[None, :]
    delta = qi + blk - kj
    band = (delta >= 0) & (delta <= wd)
    first = (jnp.arange(nblk)[:, None, None] == 0) & (kj[None] < blk)
    valid = band[None] & ~first
    bucket = t5_bucket(jnp.maximum(delta, 0) * dil)
    bias = bias_g[bucket].astype(jnp.float32).transpose(2, 0, 1)

    logits = jnp.einsum('bhrnqc,bhrnkc->bhrnqk', qb, kk).astype(jnp.float32) / math.sqrt(hd)
    logits = logits + bias[None, :, None, None]
    logits = jnp.where(valid[None, None, None], logits, -jnp.inf)
    m = jnp.max(logits, axis=-1, keepdims=True)
    p = jnp.exp(logits - m)
    den = jnp.sum(p, axis=-1, keepdims=True)
    o = jnp.einsum('bhrnqk,bhrnkc->bhrnqc', (p / den).astype(v.dtype), vv)
    lse = (m + jnp.log(den))[..., 0]

    o = o.reshape(b, h, dil, lp, hd)[:, :, :, :L].transpose(0, 3, 2, 1, 4).reshape(b, s, h, hd)
    lse = lse.reshape(b, h, dil, lp)[..., :L].transpose(0, 3, 2, 1).reshape(b, s, h)
    return o, lse


def dilated_attention(q, k, v, rel_bias):
    b, s = q.shape[:2]
    outs, lses = [], []
    for g, (window, dil) in enumerate(DIL_CONFIGS):
        bias_g = rel_bias[:, g * DIL_HEADS:(g + 1) * DIL_HEADS]
        o, lse = dilated_group(q[:, :, g], k[:, :, g], v[:, :, g], window, dil, bias_g)
        outs.append(o)
        lses.append(lse)
    wgt = jax.nn.softmax(jnp.stack(lses, axis=-1), axis=-1)
    o = jnp.stack(outs, axis=-1).astype(jnp.float32)
    o = jnp.sum(o * wgt[:, :, :, None, :], axis=-1).astype(q.dtype)
    return o.reshape(b, s, DIL_OUT_WIDTH)


def setup_inputs(seed: int = 0) -> dict:
    key = jax.random.key(seed)
    ks = jax.random.split(key, 24)
    f32 = jnp.float32

    def nrm(k, shape, fan_in):
        return jax.random.normal(k, shape, f32) * (fan_in ** -0.5)

    def gain(k, shape):
        return 1.0 + 0.02 * jax.random.normal(k, shape, f32)

    return {
        "x": jax.random.normal(ks[0], (BATCH, SEQ, D_MODEL), f32),
        "mem": jax.random.normal(ks[1], (BATCH, MEM_LEN, D_MODEL), f32),
        "a_norm": gain(ks[2], (N_A, D_MODEL)),
        "a_w_in": nrm(ks[3], (N_A, D_MODEL, A_IN_WIDTH), D_MODEL),
        "a_w_pg": nrm(ks[4], (N_A, N_POOL_GROUPS, POOL_GROUP, POOL_GROUP), POOL_GROUP),
        "a_scale": gain(ks[5], (N_A, POOL_WIDTH)),
        "a_w_out": nrm(ks[6], (N_A, A_OUT_WIDTH, D_MODEL), A_OUT_WIDTH),
        "kv_norm": gain(ks[7], (D_MODEL,)),
        "w_kv": nrm(ks[8], (D_MODEL, KV_WIDTH), D_MODEL),
        "b_norm": gain(ks[9], (N_B, D_MODEL)),
        "b_w_in": nrm(ks[10], (N_B, D_MODEL, B_IN_WIDTH), D_MODEL),
        "b_w_out": nrm(ks[11], (N_B, B_OUT_WIDTH, D_MODEL), B_OUT_WIDTH),
        "mem_norm": gain(ks[12], (D_MODEL,)),
        "w_mem_kv": nrm(ks[13], (DEPTH, D_MODEL, 2 * MEM_WIDTH), D_MODEL),
        "mlp_norm": gain(ks[14], (DEPTH, D_MODEL)),
        "mlp_w1": nrm(ks[15], (DEPTH, D_MODEL, D_FF), D_MODEL),
        "mlp_w2": nrm(ks[16], (DEPTH, D_FF, D_MODEL), D_FF),
        "rel_bias": 0.2 * jax.random.normal(ks[17], (NUM_BUCKETS, N_DIL_GROUPS * DIL_HEADS), f32),
        "final_norm": gain(ks[18], (D_MODEL,)),
    }


def reference(x, mem, a_norm, a_w_in, a_w_pg, a_scale, a_w_out, kv_norm, w_kv,
              b_norm, b_w_in, b_w_out, mem_norm, w_mem_kv, mlp_norm, mlp_w1, mlp_w2,
              rel_bias, final_norm):
    b, s, _ = x.shape
    mem_h = rmsnorm(mem, mem_norm)
    k_sh = None
    v_sh = None
    for l in range(DEPTH):
        mkv = (mem_h @ w_mem_kv[l]).reshape(b, MEM_LEN, 2, MEM_HEADS, MEM_HEAD_DIM)
        mk, mv = mkv[:, :, 0], mkv[:, :, 1]
        if l < N_A:
            h = rmsnorm(x, a_norm[l])
            u = h @ a_w_in[l]
            pool_out = pool_mixer(u[..., :POOL_WIDTH], a_w_pg[l], a_scale[l])
            mem_out = memory_attention(u[..., POOL_WIDTH:], mk, mv)
            x = x + jnp.concatenate([pool_out, mem_out], axis=-1) @ a_w_out[l]
        else:
            i = l - N_A
            if i == 0:
                kv = (rmsnorm(x, kv_norm) @ w_kv).reshape(b, s, 2, N_DIL_GROUPS, DIL_HEADS, HEAD_DIM)
                k_sh, v_sh = kv[:, :, 0], kv[:, :, 1]
            h = rmsnorm(x, b_norm[i])
            u = h @ b_w_in[i]
            q = u[..., :DIL_Q_WIDTH].reshape(b, s, N_DIL_GROUPS, DIL_HEADS, HEAD_DIM)
            dil_out = dilated_attention(q, k_sh, v_sh, rel_bias)
            mem_out = memory_attention(u[..., DIL_Q_WIDTH:], mk, mv)
            x = x + jnp.concatenate([dil_out, mem_out], axis=-1) @ b_w_out[i]
        x = x + sq_relu_mlp(rmsnorm(x, mlp_norm[l]), mlp_w1[l], mlp_w2[l])
    return rmsnorm(x, final_norm)
```

```python
import os
import math
import contextlib
import numpy as np
import concourse.bass as bass
import concourse.mybir as mybir
from concourse.bass_utils import run_bass_kernel_spmd

F32 = mybir.dt.float32
BF16 = mybir.dt.bfloat16
AF = mybir.ActivationFunctionType
ALU = mybir.AluOpType

D = 4096
S = 2048
NCH = 32
T = 512
NT = S // T
MEM = 256
DFF = 16384
EPS = 1e-6
NEG = -30000.0
NSLOT = 4
SLOT_ELEMS = 8192
DIL = ((128, 1), (512, 4), (2048, 16))

G_A, G_KV, G_B, G_MEM, G_MLP0, G_MLP1, G_FIN = range(7)


class Rec:
    ENGS = ("pe", "act", "dve", "pool", "sp")

    def __init__(self):
        self.streams = {e: [] for e in self.ENGS}
        self.cnt = {e: 0 for e in self.ENGS}
        self.seen = {e: {} for e in self.ENGS}
        self.dcnt = {}
        self.label = ""

    def _flat(self, waits, out):
        for tok in waits:
            if tok is None:
                continue
            if isinstance(tok, list):
                self._flat(tok, out)
            else:
                key, val = tok
                if val > out.get(key, 0):
                    out[key] = val
        return out

    def _waits(self, eng, waits):
        st = self.streams[eng]
        seen = self.seen[eng]
        for key, val in self._flat(waits, {}).items():
            if key == "pe" and eng == "pe":
                continue
            if seen.get(key, 0) >= val:
                continue
            seen[key] = val
            st.append(("wait", key, val, self.label))

    def op(self, eng, fn, waits=(), signal=True):
        self._waits(eng, waits)
        if signal:
            self.cnt[eng] += 1
            tok = (eng, self.cnt[eng])
            self.streams[eng].append(("op", fn, (eng, 1)))
            return tok
        self.streams[eng].append(("op", fn, None))
        return None

    def dma(self, eng, out, in_, dsem, waits=()):
        self._waits(eng, waits)
        self.dcnt[dsem] = self.dcnt.get(dsem, 0) + 16
        self.streams[eng].append(("op", lambda e, o=out, i=in_: e.dma_start(out=o, in_=i), (dsem, 16)))
        return (dsem, self.dcnt[dsem])

    def wait(self, eng, waits):
        self._waits(eng, waits)


class K:
    pass


def build_program(debug=False):
    nc = bass.Bass("TRN2", target_bir_lowering=False)
    R = Rec()
    k = K()

    def din(name, shape, dt=F32):
        return nc.dram_tensor(name, list(shape), dt, kind="ExternalInput").ap()

    def dscr(name, shape, dt):
        kind = "ExternalOutput" if debug else "Internal"
        return nc.dram_tensor(name, list(shape), dt, kind=kind).ap()

    xT = din("xT", [D, S])
    memT = din("memT", [D, MEM])
    gains = din("gains", [128, 7 * NCH])
    ascale = din("ascale", [128, 24])
    invcnt = din("invcnt", [128, 4 * 16])
    biasT = din("biasT", [128, 24 * 256])
    a_w_in = din("a_w_in", [D, D])
    a_w_pg = din("a_w_pg", [4 * 768, 768])
    a_w_out = din("a_w_out", [D, D])
    w_kv = din("w_kv", [D, 6144])
    b_w_in = din("b_w_in", [D, D])
    b_w_out = din("b_w_out", [2048, D])
    w_mem_kv = din("w_mem_kv", [2 * D, 2048])
    mlp_w1 = din("mlp_w1", [2 * D, DFF])
    mlp_w2 = din("mlp_w2", [2 * DFF, D])
    outT = nc.dram_tensor("outT", [D, S], F32, kind="ExternalOutput").ap()

    x1T = dscr("x1T", [D, S], F32)
    xmT = dscr("xmT", [D, S], F32)
    x2T = dscr("x2T", [D, S], F32)
    x3T = dscr("x3T", [D, S], F32)
    xnT = dscr("xnT", [D, S], F32)
    x4T = dscr("x4T", [D, S], F32)
    KTd = dscr("KTd", [3072, S], BF16)
    Vd = dscr("Vd", [S, 3072], BF16)
    QTd = dscr("QTd", [3072, S], BF16)
    MO1 = dscr("MO1", [1024, S], BF16)
    ATT = dscr("ATT", [1024, S], BF16)

    es = contextlib.ExitStack()

    def sb(name, shape, dt):
        return es.enter_context(nc.sbuf_tensor(name, list(shape), dt))

    ring = [sb(f"ring{i}", [128, SLOT_ELEMS], BF16) for i in range(NSLOT)]
    hT = sb("hT", [128, NCH * T], BF16)
    big = sb("big", [128, 16384], BF16)
    qm = sb("qm", [128, 8 * T], BF16)
    stg = [sb(f"stg{i}", [128, 2 * T], F32) for i in range(2)]
    sq = sb("sq", [128, 2 * T], F32)
    acc = sb("acc", [128, 2 * T], F32)
    acc1 = sb("acc1", [128, T], F32)
    rt = sb("rt", [128, T], F32)
    rstd = sb("rstd", [128, T], F32)
    mkT = sb("mkT", [128, 2 * 8 * MEM], BF16)
    mv = sb("mv", [128, 2 * 2 * 1024], BF16)
    gains_sb = sb("gains_sb", [128, 7 * NCH], F32)
    ascale_sb = sb("ascale_sb", [128, 24], F32)
    invcnt_sb = sb("invcnt_sb", [128, 64], F32)
    ones_bf = sb("ones_bf", [128, 128], BF16)
    ones_f = sb("ones_f", [128, 128], F32)
    xst = [sb(f"xst{i}", [128, T], F32) for i in range(2)]
    ost = [sb(f"ost{i}", [128, T], F32) for i in range(2)]
    bst = [sb(f"bst{i}", [128, T], BF16) for i in range(3)]
    ubuf = [sb(f"ubuf{i}", [128, 16 + T], F32) for i in range(2)]
    sA = sb("sA", [128, 16 + T], F32)
    sB = sb("sB", [128, 16 + T], F32)
    halo = sb("halo", [128, 24 * 16], F32)
    fix16 = sb("fix16", [128, 16], F32)
    ftmp = [ubuf[0][:, 0:T], ubuf[1][:, 0:T]]
    xstx = [sb(f"xstx{i}", [128, T], F32) for i in range(2)]
    pt_sb = [sb(f"pt{i}", [128, 2 * T], BF16) for i in range(2)]
    rden = [sb(f"rden{i}", [128, T], F32) for i in range(1)]

    psum = [es.enter_context(nc.psum_tensor(f"ps{i}", [128, T], F32)) for i in range(8)]

    class Rot:
        def __init__(self, bufs):
            self.bufs = bufs
            self.i = 0
            self.guard = [[] for _ in bufs]

        def next(self):
            i = self.i
            self.i = (self.i + 1) % len(self.bufs)
            g = self.guard[i]
            self.guard[i] = []
            return i, self.bufs[i], g

        def release(self, i, tok):
            self.guard[i].append(tok)

    banks = Rot(psum[:7])
    STAT = psum[7]
    ost_r = Rot(ost)
    bst_r = Rot(bst)
    ftmp_r = Rot(ftmp)
    stg_r = Rot(stg)
    ubuf_r = Rot(ubuf)
    pt_r = Rot(pt_sb)
    rden_r = Rot(rden)

    def sp_dma_toks():
        return [(key, v) for key, v in R.dcnt.items() if not key.startswith("w")]

    def barrier():
        toks = [(e, R.cnt[e]) for e in ("pe", "act", "dve") if R.cnt[e] > 0] + sp_dma_toks()
        for e in ("pe", "act", "dve", "sp"):
            R.wait(e, [t for t in toks if t[0] != e])

    wstate = {"n": 0, "free": [[] for _ in range(NSLOT)]}

    def wget(src_ap, kc, cols):
        assert kc * cols <= SLOT_ELEMS
        s_ = wstate["n"] % NSLOT
        wstate["n"] += 1
        view = ring[s_][:, 0:kc * cols].rearrange("p (k c) -> p k c", c=cols)
        g = wstate["free"][s_]
        wstate["free"][s_] = []
        tok = R.dma("pool", view, src_ap, f"w{s_}", waits=g)
        return s_, view, tok

    def wdone(s_, tok):
        wstate["free"][s_].append(tok)

    def wsrc(w, r0, kc, c0, cols):
        return w[r0:r0 + kc * 128, c0:c0 + cols].rearrange("(k p) c -> p k c", p=128)

    def mm(out, lhsT, rhs, start, stop, waits=(), signal=False):
        return R.op("pe", lambda e: e.matmul(out, lhsT, rhs, start=start, stop=stop), waits=waits, signal=signal)

    def act(out, in_, func, waits=(), scale=1.0, bias=None):
        if bias is None:
            return R.op("act", lambda e: e.activation(out=out, in_=in_, func=func, scale=scale), waits=waits)
        return R.op("act", lambda e: e.activation(out=out, in_=in_, func=func, bias=bias, scale=scale), waits=waits)

    def tt(out, in0, in1, op, waits=(), eng="dve"):
        return R.op(eng, lambda e: e.tensor_tensor(out=out, in0=in0, in1=in1, op=op), waits=waits)

    def stt(out, in0, scalar, in1, op0, op1, waits=(), eng="dve"):
        return R.op(eng, lambda e: e.scalar_tensor_tensor(out=out, in0=in0, scalar=scalar, in1=in1, op0=op0, op1=op1),
                    waits=waits)

    def cp(out, in_, waits=(), eng="dve"):
        return R.op(eng, lambda e: e.tensor_copy(out=out, in_=in_), waits=waits)

    def recip(out, in_, waits=()):
        return R.op("dve", lambda e: e.reciprocal(out=out, in_=in_), waits=waits)

    def memset(ap, val, waits=()):
        return R.op("dve", lambda e: e.memset(ap, val), waits=waits)

    R.dma("sp", gains_sb[:], gains[:], "cst")
    R.dma("sp", ascale_sb[:], ascale[:], "cst")
    CST = R.dma("sp", invcnt_sb[:], invcnt[:], "cst")
    eps_sb = sb("eps_sb", [128, 1], F32)
    ONES = [memset(ones_f[:], 1.0), memset(ones_bf[:], 1.0), memset(halo[:], 0.0), memset(eps_sb[:], EPS)]

    def gain_ap(gi, c):
        return gains_sb[:, gi * NCH + c: gi * NCH + c + 1]

    ns = {"acc_g": [], "acc1_g": [], "rt_g": [], "stat_g": [], "rstd_g": []}

    def rms_stats(src, c0, n, split=False):
        last = None
        for cg in range(NCH // 2):
            si, sbuf_, g = stg_r.next()
            sv = sbuf_[:, 0:2 * n].rearrange("p (c t) -> p c t", t=n)
            ld = R.dma("sp", sv, src[cg * 256:(cg + 1) * 256, c0:c0 + n].rearrange("(c p) t -> p c t", p=128),
                       f"stg{si}", waits=g)
            if cg == 0:
                t1 = act(acc[:, 0:2 * n], sbuf_[:, 0:2 * n], AF.Square, waits=[ld] + ns["acc_g"])
                stg_r.release(si, t1)
                last = t1
            else:
                t1 = act(sq[:, 0:2 * n], sbuf_[:, 0:2 * n], AF.Square, waits=[ld, last])
                stg_r.release(si, t1)
                last = tt(acc[:, 0:2 * n], acc[:, 0:2 * n], sq[:, 0:2 * n], ALU.add, waits=[t1, last])
        t2 = tt(acc1[:, 0:n], acc[:, 0:n], acc[:, n:2 * n], ALU.add, waits=[last] + ONES + ns["acc1_g"])
        ns["acc_g"] = [t2]
        if split:
            return t2
        return rms_stats_b(t2, n)

    def rms_stats_b(t2, n):
        t3 = mm(STAT[:, 0:n], ones_f[:], acc1[:, 0:n], True, True, waits=[t2] + ns["stat_g"], signal=True)
        ns["acc1_g"] = [t3]
        t4 = act(rt[:, 0:n], STAT[:, 0:n], AF.Sqrt, waits=[t3] + ns["rt_g"], scale=1.0 / D, bias=eps_sb[:, 0:1])
        ns["stat_g"] = [t4]
        t5 = recip(rstd[:, 0:n], rt[:, 0:n], waits=[t4] + ns["rstd_g"])
        ns["rt_g"] = [t5]
        ns["rstd_g"] = []
        return t5

    def rms_apply(src, c0, n, rstd_tok, outs, h_guard):
        tk = None
        for cg in range(NCH // 2):
            si, sbuf_, g = stg_r.next()
            sv = sbuf_[:, 0:2 * n].rearrange("p (c t) -> p c t", t=n)
            ld = R.dma("sp", sv, src[cg * 256:(cg + 1) * 256, c0:c0 + n].rearrange("(c p) t -> p c t", p=128),
                       f"stg{si}", waits=g)
            for cc in range(2):
                c = cg * 2 + cc
                for (gi, dst) in outs:
                    tk = stt(dst(c), sbuf_[:, cc * n:(cc + 1) * n], gain_ap(gi, c), rstd[:, 0:n], ALU.mult, ALU.mult,
                             waits=[ld, rstd_tok, CST] + h_guard)
            stg_r.release(si, tk)
        ns["rstd_g"].append(tk)
        return tk

    def proj_A(w, r0, nk, c0, ncols, rhs_fn, rhs_tok, n, evac, cols=512, kcu=16):
        assert kcu * cols <= SLOT_ELEMS and nk % kcu == 0 and ncols % cols == 0
        mpu = cols // 128
        nu = nk // kcu
        for gidx in range(ncols // cols):
            bb = [banks.next() for _ in range(mpu)]
            allg = [b_[2] for b_ in bb]
            for u in range(nu):
                s_, wv, wtok = wget(wsrc(w, r0 + u * kcu * 128, kcu, c0 + gidx * cols, cols), kcu, cols)
                tok = None
                for c in range(mpu):
                    bi, bank, g = bb[c]
                    for kk in range(kcu):
                        kabs = u * kcu + kk
                        tok = mm(bank[:, 0:n], wv[:, kk, c * 128:(c + 1) * 128], rhs_fn(kabs), kabs == 0, kabs == nk - 1,
                                 waits=([wtok, rhs_tok] + (allg if u == 0 else [])) if kk == 0 else (),
                                 signal=(kk == kcu - 1))
                    if u == nu - 1:
                        et = evac(gidx * mpu + c, bank[:, 0:n], tok)
                        banks.release(bi, et)
                wdone(s_, tok)

    def mem_attention(l, q_fn, q_tok, dst_fn, after):
        for h in range(4):
            pi, pt, pg = pt_r.next()
            etoks = []
            for mch in range(2):
                bi, bank, g = banks.next()
                tok = None
                for cc in range(2):
                    base = (l * 8 + h * 2 + cc) * MEM + mch * 128
                    tok = mm(bank[:], mkT[:, base:base + 128], q_fn(h * 2 + cc), cc == 0, cc == 1,
                             waits=([q_tok, MEMTOK] + g) if cc == 0 else (), signal=(cc == 1))
                et = act(pt[:, mch * T:(mch + 1) * T], bank[:], AF.Exp, waits=[tok] + pg, scale=1.0 / 16.0)
                banks.release(bi, et)
                etoks.append(et)
            bi, bank, g = banks.next()
            tok = None
            for mch in range(2):
                tok = mm(bank[:], ones_bf[:], pt[:, mch * T:(mch + 1) * T], mch == 0, mch == 1,
                         waits=(etoks + g + ONES) if mch == 0 else (), signal=(mch == 1))
            ri, rd, rg = rden_r.next()
            rtok = recip(rd[:], bank[:], waits=[tok] + rg)
            banks.release(bi, rtok)
            last = None
            for oc in range(2):
                bi, bank, g = banks.next()
                tok = None
                for mch in range(2):
                    base = (l * 2 + mch) * 1024 + h * 256 + oc * 128
                    tok = mm(bank[:], mv[:, base:base + 128], pt[:, mch * T:(mch + 1) * T], mch == 0, mch == 1,
                             waits=(etoks + g) if mch == 0 else (), signal=(mch == 1))
                dst, dg = dst_fn(h * 2 + oc)
                ot = tt(dst, bank[:], rd[:], ALU.mult, waits=[tok, rtok] + dg)
                banks.release(bi, ot)
                after(h * 2 + oc, ot)
                last = ot
            rden_r.release(ri, last)
            pt_r.release(pi, tok)

    xs = {"r": Rot([b_[:] for b_ in xst])}

    def resid_evac(src, dst, c0, nchunks=NCH):
        xr = xs["r"]
        depth = len(xr.bufs) - 1
        pending = {}

        def issue(m):
            xi, xb, xg = xr.next()
            ld = R.dma("sp", xb, src[m * 128:(m + 1) * 128, c0:c0 + T], f"xst{xi}", waits=xg)
            pending[m] = (xi, xb, ld)

        for m in range(min(depth, nchunks)):
            issue(m)

        def evac(m, ps, tok):
            xi, xb, ld = pending.pop(m)
            oi, ob, og = ost_r.next()
            et = tt(ob[:], ps, xb, ALU.add, waits=[tok, ld] + og)
            xr.release(xi, et)
            if m + depth < nchunks:
                issue(m + depth)
            st = R.dma("sp", dst[m * 128:(m + 1) * 128, c0:c0 + T], ob[:], f"ost{oi}", waits=[et])
            ost_r.release(oi, st)
            return et
        return evac

    def store_bf(dst_ap, make):
        i, b, g = bst_r.next()
        et = make(b[:], g)
        st = R.dma("sp", dst_ap, b[:], f"bst{i}", waits=[et])
        bst_r.release(i, st)
        return et

    def pe_now():
        return [("pe", R.cnt["pe"])]

    R.label = "M"
    hmem = big[:, 0:NCH * MEM]
    rs = rms_stats(memT, 0, MEM)
    hm_tok = rms_apply(memT, 0, MEM, rs, [(G_MEM, lambda c: hmem[:, c * MEM:(c + 1) * MEM])], [])
    for l in range(2):
        def ev_k(m, ps, tok, l=l):
            return act(mkT[:, (l * 8 + m) * MEM:(l * 8 + m + 1) * MEM], ps, AF.Copy, waits=[tok])
        proj_A(w_mem_kv, l * D, NCH, 0, 1024, lambda kk: hmem[:, kk * MEM:(kk + 1) * MEM], hm_tok, MEM, ev_k)
        for cg in range(2):
            bb = [banks.next() for _ in range(2)]
            tok = None
            for u in range(2):
                s_, wv, wtok = wget(wsrc(w_mem_kv, l * D + u * 2048, 16, 1024 + cg * 512, 512), 16, 512)
                for mch in range(2):
                    bi, bank, g = bb[mch]
                    for kk in range(16):
                        kabs = u * 16 + kk
                        tok = mm(bank[:], hmem[:, kabs * MEM + mch * 128: kabs * MEM + mch * 128 + 128], wv[:, kk, :],
                                 kabs == 0, kabs == NCH - 1,
                                 waits=([wtok, hm_tok] + (g if u == 0 else [])) if kk == 0 else (),
                                 signal=(kk == 15))
                wdone(s_, tok)
            for mch in range(2):
                bi, bank, g = bb[mch]
                base = (l * 2 + mch) * 1024 + cg * 512
                et = act(mv[:, base:base + 512], bank[:], AF.Copy, waits=[tok])
                banks.release(bi, et)
    MEMTOK = ("act", R.cnt["act"])
    barrier()

    pooledT = big[:, 0:24 * T]
    concat = hT
    qmem = qm
    h_guard = []
    halo_tok = {}
    t2n = rms_stats(xT, 0, T, split=True)
    for j in range(NT):
        c0 = j * T
        R.label = "A.norm"
        rs = rms_stats_b(t2n, T)
        h_tok = rms_apply(xT, c0, T, rs, [(G_A, lambda c: hT[:, c * T:(c + 1) * T])], h_guard)
        state = {"pooled_last": None, "q_last": None, "big_guard": list(h_guard), "s_guard": []}

        def ev_in(m, ps, tok, j=j, state=state):
            if m >= 24:
                t_ = act(qmem[:, (m - 24) * T:(m - 23) * T], ps, AF.Copy, waits=[tok] + state["big_guard"])
                state["q_last"] = t_
                return t_
            g_ = m // 6
            L = g_ + 1
            w_ = 2 ** L
            ui, ub, ug = ubuf_r.next()
            t_ = act(ub[:, 16:16 + T], ps, AF.Copy, waits=[tok] + ug)
            if j == 0:
                th = memset(ub[:, 0:16], 0.0, waits=ug)
            else:
                th = cp(ub[:, 0:16], halo[:, m * 16:(m + 1) * 16], waits=ug + [halo_tok[m]])
            src_ = ub
            pp = [sA, sB]
            lastt = [t_, th]
            for lv in range(1, L + 1):
                sh = 2 ** (lv - 1)
                lo = 2 ** lv - 1
                dst_ = pp[(lv - 1) % 2]
                tl = tt(dst_[:, lo:16 + T], src_[:, lo:16 + T], src_[:, lo - sh:16 + T - sh], ALU.add,
                        waits=lastt + state["s_guard"])
                lastt = [tl]
                src_ = dst_
            tf = stt(pooledT[:, m * T:(m + 1) * T], src_[:, 16:16 + T], 1.0 / w_, ub[:, 16:16 + T], ALU.mult,
                     ALU.subtract, waits=lastt + state["big_guard"])
            if j == 0:
                tf1 = tt(fix16[:, 0:16], src_[:, 16:32], invcnt_sb[:, g_ * 16:(g_ + 1) * 16], ALU.mult,
                         waits=[tf, CST])
                tf = tt(pooledT[:, m * T:m * T + 16], fix16[:, 0:16], ub[:, 16:32], ALU.subtract, waits=[tf1])
            th2 = cp(halo[:, m * 16:(m + 1) * 16], ub[:, T:T + 16], waits=[tf])
            halo_tok[m] = th2
            ubuf_r.release(ui, th2)
            state["s_guard"] = [tf]
            state["pooled_last"] = th2
            return t_

        R.label = "A.in"
        proj_A(a_w_in, 0, NCH, 0, D, lambda kk: hT[:, kk * T:(kk + 1) * T], h_tok, T, ev_in)
        cc_tok = {}
        R.label = "A.pg"
        for g_ in range(4):
            s_, wv, wtok = wget(wsrc(a_w_pg, g_ * 768, 6, 0, 768), 6, 768)
            tok = None
            for dc in range(6):
                bi, bank, bg = banks.next()
                for cc in range(6):
                    tok = mm(bank[:], wv[:, cc, dc * 128:(dc + 1) * 128],
                             pooledT[:, (g_ * 6 + cc) * T:(g_ * 6 + cc + 1) * T],
                             cc == 0, cc == 5, waits=([wtok, state["pooled_last"]] + bg) if cc == 0 else (),
                             signal=(cc == 5))
                m = g_ * 6 + dc
                et = act(concat[:, m * T:(m + 1) * T], bank[:], AF.Copy, waits=[tok, CST],
                         scale=ascale_sb[:, m:m + 1])
                banks.release(bi, et)
                cc_tok[m] = et
            wdone(s_, tok)
        R.label = "A.mem"
        mem_attention(0, lambda c: qmem[:, c * T:(c + 1) * T], state["q_last"],
                      lambda c: (concat[:, (24 + c) * T:(25 + c) * T], []),
                      lambda c, tok: cc_tok.__setitem__(24 + c, tok))
        cat_tok = [cc_tok[m] for m in range(32)]
        if j + 1 < NT:
            R.label = "A.norm"
            t2n = rms_stats(xT, c0 + T, T, split=True)
        R.label = "A.out"
        proj_A(a_w_out, 0, NCH, 0, D, lambda kk: concat[:, kk * T:(kk + 1) * T], cat_tok, T,
               resid_evac(xT, x1T, c0))
        h_guard = pe_now()
    barrier()

    xst_deep_aps = [xst[0][:], xst[1][:], sA[:, 0:T], sB[:, 0:T], xstx[0][:], xstx[1][:]]
    FFG = 32
    NQ = DFF // (FFG * 128)

    def mlp_phase(l, gi, x_in, mids, x_out):
        aT = big
        xs["r"] = Rot(list(xst_deep_aps))
        a_guard = []
        rs = rms_stats(x_in, 0, T)
        h_tok = rms_apply(x_in, 0, T, rs, [(gi, lambda c: hT[:, c * T:(c + 1) * T])], [])
        for j in range(NT):
            c0 = j * T
            h_next = None
            for q in range(NQ):
                a_last = {}
                ag = list(a_guard)

                def ev1(m, ps, tok, ag=ag, a_last=a_last):
                    fi, fb, fg = ftmp_r.next()
                    t1 = act(fb, ps, AF.Relu, waits=[tok] + fg)
                    t2 = tt(aT[:, m * T:(m + 1) * T], fb, fb, ALU.mult, waits=[t1] + ag)
                    ftmp_r.release(fi, t2)
                    a_last["t"] = t2
                    return t1

                R.label = f"mlp{l}.up"
                proj_A(mlp_w1, l * D, NCH, q * FFG * 128, FFG * 128, lambda kk: hT[:, kk * T:(kk + 1) * T], h_tok, T, ev1)
                a_tok = a_last["t"]
                R.label = f"mlp{l}.norm"
                if q == NQ - 2 and j + 1 < NT:
                    t2n = rms_stats(x_in, c0 + T, T, split=True)
                if q == NQ - 1 and j + 1 < NT:
                    rs = rms_stats_b(t2n, T)
                    h_next = rms_apply(x_in, c0 + T, T, rs, [(gi, lambda c: hT[:, c * T:(c + 1) * T])], pe_now())
                src = x_in if q == 0 else mids[(q - 1) % 2]
                dst = x_out if q == NQ - 1 else mids[q % 2]
                if q > 0:
                    R.wait("sp", sp_dma_toks())
                R.label = f"mlp{l}.down"
                ev2 = resid_evac(src, dst, c0)
                for dg in range(8):
                    bb = [banks.next() for _ in range(4)]
                    allg = [b_[2] for b_ in bb]
                    for u in range(2):
                        s_, wv, wtok = wget(wsrc(mlp_w2, l * DFF + q * FFG * 128 + u * 2048, 16, dg * 512, 512), 16, 512)
                        tok = None
                        for c in range(4):
                            bi, bank, g = bb[c]
                            for kk in range(16):
                                kabs = u * 16 + kk
                                tok = mm(bank[:], wv[:, kk, c * 128:(c + 1) * 128], aT[:, kabs * T:(kabs + 1) * T],
                                         kabs == 0, kabs == FFG - 1,
                                         waits=([wtok, a_tok] + (allg if u == 0 else [])) if kk == 0 else (),
                                         signal=(kk == 15))
                            if u == 1:
                                et = ev2(dg * 4 + c, bank[:], tok)
                                banks.release(bi, et)
                        wdone(s_, tok)
                a_guard = pe_now()
            h_tok = h_next

    mlp_phase(0, G_MLP0, x1T, (xmT, xnT), x2T)
    barrier()

    hT2 = big
    qmem1 = qm
    rs = rms_stats(x2T, 0, T)
    hk_tok = rms_apply(x2T, 0, T, rs, [(G_KV, lambda c: hT[:, c * T:(c + 1) * T])], [])
    hb_tok = rms_apply(x2T, 0, T, rs, [(G_B, lambda c: hT2[:, c * T:(c + 1) * T])], [])
    for j in range(NT):
        c0 = j * T
        R.label = "C"
        h_tok = [hk_tok, hb_tok]

        def ev_kT(m, ps, tok, c0=c0):
            return store_bf(KTd[m * 128:(m + 1) * 128, c0:c0 + T],
                            lambda b, g: act(b, ps, AF.Copy, waits=[tok] + g))

        proj_A(w_kv, 0, NCH, 0, 3072, lambda kk: hT[:, kk * T:(kk + 1) * T], hk_tok, T, ev_kT)
        for cg in range(6):
            bb = [banks.next() for _ in range(4)]
            tok = None
            for u in range(2):
                s_, wv, wtok = wget(wsrc(w_kv, u * 2048, 16, 3072 + cg * 512, 512), 16, 512)
                for tc in range(4):
                    bi, bank, g = bb[tc]
                    for kk in range(16):
                        kabs = u * 16 + kk
                        tok = mm(bank[:], hT[:, kabs * T + tc * 128: kabs * T + tc * 128 + 128], wv[:, kk, :],
                                 kabs == 0, kabs == NCH - 1,
                                 waits=([wtok, hk_tok] + (g if u == 0 else [])) if kk == 0 else (),
                                 signal=(kk == 15))
                wdone(s_, tok)
            for tc in range(4):
                bi, bank, g = bb[tc]
                et = store_bf(Vd[c0 + tc * 128:c0 + (tc + 1) * 128, cg * 512:(cg + 1) * 512],
                              lambda b, gg, bank=bank, tok=tok: act(b, bank[:], AF.Copy, waits=[tok] + gg))
                banks.release(bi, et)
        if j + 1 < NT:
            rs = rms_stats(x2T, c0 + T, T)
            hk_tok = rms_apply(x2T, c0 + T, T, rs, [(G_KV, lambda c: hT[:, c * T:(c + 1) * T])], pe_now())
        qstate = {"q_last": None}

        def ev_q(m, ps, tok, c0=c0, qstate=qstate):
            if m >= 24:
                t_ = act(qmem1[:, (m - 24) * T:(m - 23) * T], ps, AF.Copy, waits=[tok])
                qstate["q_last"] = t_
                return t_
            return store_bf(QTd[m * 128:(m + 1) * 128, c0:c0 + T],
                            lambda b, g: act(b, ps, AF.Copy, waits=[tok] + g))

        proj_A(b_w_in, 0, NCH, 0, D, lambda kk: hT2[:, kk * T:(kk + 1) * T], hb_tok, T, ev_q)
        if j + 1 < NT:
            hb_tok = rms_apply(x2T, c0 + T, T, rs, [(G_B, lambda c: hT2[:, c * T:(c + 1) * T])], pe_now())
        pend = {}

        def mo_dst(c, pend=pend):
            i, b, g = bst_r.next()
            pend[c] = (i, b)
            return b[:], g

        def mo_after(c, tok, pend=pend, c0=c0):
            i, b = pend[c]
            st = R.dma("sp", MO1[c * 128:(c + 1) * 128, c0:c0 + T], b[:], f"bst{i}", waits=[tok])
            bst_r.release(i, st)

        mem_attention(1, lambda c: qmem1[:, c * T:(c + 1) * T], qstate["q_last"], mo_dst, mo_after)
    barrier()

    R.label = "D"
    qkv = []
    bt_t = []
    for i in range(2):
        o = i * 6656
        qkv.append((big[:, o:o + 2048], big[:, o + 2048:o + 4096], big[:, o + 4096:o + 6144]))
        bt_t.append(big[:, o + 6144:o + 6656].bitcast(F32))
    accn = hT[:, 0:4096].bitcast(F32)
    accd = hT[:, 4096:8192].bitcast(F32)
    tmpS = [hT[:, 8192 + i * 1024: 8192 + (i + 1) * 1024].bitcast(F32) for i in range(2)]
    ptd = [hT[:, 10240 + i * 512: 10240 + (i + 1) * 512] for i in range(2)]
    oh = [hT[:, 11264 + i * 2048: 11264 + (i + 1) * 2048] for i in range(2)]
    qkv_r = Rot([(q_[0], q_[1], q_[2], bt_t[i]) for i, q_ in enumerate(qkv)])
    oh_r = Rot(oh)
    SC = 1.0 / math.sqrt(128.0)
    tmpS_r = Rot(tmpS + [sq[:, 0:T], sq[:, T:2 * T]])
    ptd_r = Rot(ptd + [qm[:, 0:T], qm[:, T:2 * T]])

    steps_all = []
    for h in range(8):
        for g_, (win, d) in enumerate(DIL):
            nblk = (S // d) // 128
            full = [(r, n) for r in range(d) for n in range(1, nblk)]
            first = [(r, 0) for r in range(d)]
            st_list = [("full", full[i:i + 2]) for i in range(0, len(full), 2)] + \
                      [("first", first[i:i + 4]) for i in range(0, len(first), 4)]
            for si_, (kind, blks) in enumerate(st_list):
                steps_all.append(dict(h=h, g=g_, d=d, nblk=nblk, kind=kind, blks=blks, first_gh=(si_ == 0),
                                      last_gh=(si_ == len(st_list) - 1), last_head=(g_ == 2 and si_ == len(st_list) - 1)))

    cur = {}
    dstate = {"acc_last": None, "acc_guard": []}

    def tokslice(r, n, d):
        st_ = r + d * 128 * n
        return slice(st_, st_ + 127 * d + 1, d)

    def stage1(sp_):
        h, g_, d, nblk, kind, blks = sp_["h"], sp_["g"], sp_["d"], sp_["nblk"], sp_["kind"], sp_["blks"]
        gh = g_ * 8 + h
        if sp_["first_gh"]:
            qi_, (qh, kh, vh, bth), qg = qkv_r.next()
            R.dma("sp", qh, QTd[gh * 128:(gh + 1) * 128, :], f"qkv{qi_}", waits=qg)
            R.dma("sp", kh, KTd[gh * 128:(gh + 1) * 128, :], f"qkv{qi_}")
            ld = R.dma("sp", bth, biasT[:, gh * 256:(gh + 1) * 256], f"qkv{qi_}")
            vsrc = Vd[:, gh * 128:(gh + 1) * 128].rearrange("(n i r) c -> i r n c", i=128, r=d)
            vdst = vh.rearrange("p (r n c) -> p r n c", r=d, n=nblk)
            for r in range(d):
                ld = R.dma("sp", vdst[:, r], vsrc[:, r], f"qkv{qi_}")
            cur.update(qi=qi_, qh=qh, kh=kh, vh=vh, bth=bth, ld=ld)
        sp_.update(qi=cur["qi"], qh=cur["qh"], kh=cur["kh"], vh=cur["vh"], bth=cur["bth"], ld=cur["ld"])
        qh, kh, bth, ld = sp_["qh"], sp_["kh"], sp_["bth"], sp_["ld"]
        nh = 2 if kind == "full" else 1
        wdt = nh * 128
        bi, bank, bg = banks.next()
        tok = None
        first_mm = True
        for b_, (r, n) in enumerate(blks):
            for hf in range(nh):
                kn = (n - 1 + hf) if kind == "full" else n
                tok = mm(bank[:, b_ * wdt + hf * 128: b_ * wdt + hf * 128 + 128],
                         kh[:, tokslice(r, kn, d)], qh[:, tokslice(r, n, d)], True, True,
                         waits=([ld] + bg) if first_mm else (),
                         signal=(b_ == len(blks) - 1 and hf == nh - 1))
                first_mm = False
        W_ = len(blks) * wdt
        ti, tS, tg = tmpS_r.next()
        tl = None
        boff = 0 if kind == "full" else 128
        for b_ in range(len(blks)):
            tl = stt(tS[:, b_ * wdt:(b_ + 1) * wdt], bank[:, b_ * wdt:(b_ + 1) * wdt], SC,
                     bth[:, boff:boff + wdt], ALU.mult, ALU.add, waits=[tok, ld] + tg)
        banks.release(bi, tl)
        pi, pt, pg = ptd_r.next()
        te = act(pt[:, 0:W_], tS[:, 0:W_], AF.Exp, waits=[tl] + pg)
        tmpS_r.release(ti, te)
        sp_.update(pi=pi, pt=pt, te=te, nh=nh, wdt=wdt)

    def stage2(sp_):
        h, g_, d, nblk, kind, blks = sp_["h"], sp_["g"], sp_["d"], sp_["nblk"], sp_["kind"], sp_["blks"]
        vh, pt, te, nh, wdt = sp_["vh"], sp_["pt"], sp_["te"], sp_["nh"], sp_["wdt"]
        bn, bankn, gn = banks.next()
        bd, bankd, gd = banks.next()
        tok2 = None
        first_mm = True
        for b_, (r, n) in enumerate(blks):
            for hf in range(nh):
                kn = (n - 1 + hf) if kind == "full" else n
                vb = (r * nblk + kn) * 128
                mm(bankn[:, b_ * 128:(b_ + 1) * 128], vh[:, vb:vb + 128],
                   pt[:, b_ * wdt + hf * 128: b_ * wdt + hf * 128 + 128], hf == 0, hf == nh - 1,
                   waits=([te] + gn + gd + ONES) if first_mm else ())
                first_mm = False
        for b_, (r, n) in enumerate(blks):
            for hf in range(nh):
                tok2 = mm(bankd[:, b_ * 128:(b_ + 1) * 128], ones_bf[:],
                          pt[:, b_ * wdt + hf * 128: b_ * wdt + hf * 128 + 128], hf == 0, hf == nh - 1,
                          signal=(b_ == len(blks) - 1 and hf == nh - 1))
        ptd_r.release(sp_["pi"], tok2)
        ta = None
        for b_, (r, n) in enumerate(blks):
            sl = tokslice(r, n, d)
            if g_ == 0:
                ta = cp(accn[:, sl], bankn[:, b_ * 128:(b_ + 1) * 128], waits=[tok2] + dstate["acc_guard"])
                ta = cp(accd[:, sl], bankd[:, b_ * 128:(b_ + 1) * 128], waits=[ta])
            else:
                ta = tt(accn[:, sl], accn[:, sl], bankn[:, b_ * 128:(b_ + 1) * 128], ALU.add,
                        waits=[tok2, dstate["acc_last"]])
                ta = tt(accd[:, sl], accd[:, sl], bankd[:, b_ * 128:(b_ + 1) * 128], ALU.add, waits=[ta])
            dstate["acc_last"] = ta
        banks.release(bn, ta)
        banks.release(bd, ta)
        if sp_["last_gh"]:
            qkv_r.release(sp_["qi"], tok2)
        if sp_["last_head"]:
            t1 = recip(accd, accd, waits=[dstate["acc_last"]])
            oi_, ob, og = oh_r.next()
            t2 = tt(ob, accn, accd, ALU.mult, waits=[t1] + og)
            dstate["acc_guard"] = [t2]
            st = R.dma("sp", ATT[h * 128:(h + 1) * 128, :], ob, f"oh{oi_}", waits=[t2])
            oh_r.release(oi_, st)

    PIPE = 2
    for i_, sp_ in enumerate(steps_all):
        stage1(sp_)
        if i_ >= PIPE:
            stage2(steps_all[i_ - PIPE])
    for sp_ in steps_all[-PIPE:]:
        stage2(sp_)
    barrier()

    R.label = "E"
    xs["r"] = Rot(list(xst_deep_aps))
    cats = [big[:, 0:16 * T], big[:, 16 * T:32 * T]]
    cat_guard = [[], []]

    def cat_load(j):
        cv = cats[j % 2].rearrange("p (c t) -> p c t", t=T)
        R.dma("sp", cv[:, 0:8, :], ATT[:, j * T:(j + 1) * T].rearrange("(c p) t -> p c t", p=128), f"cat{j % 2}",
              waits=cat_guard[j % 2])
        return R.dma("sp", cv[:, 8:16, :], MO1[:, j * T:(j + 1) * T].rearrange("(c p) t -> p c t", p=128), f"cat{j % 2}")

    l2 = cat_load(0)
    for j in range(NT):
        c0 = j * T
        cat1 = cats[j % 2]
        l2n = cat_load(j + 1) if j + 1 < NT else None
        ev = resid_evac(x2T, x3T, c0)
        for u in range(8):
            s_, wv, wtok = wget(wsrc(b_w_out, 0, 16, u * 512, 512), 16, 512)
            tok = None
            for c in range(4):
                bi, bank, g = banks.next()
                for kk in range(16):
                    tok = mm(bank[:], wv[:, kk, c * 128:(c + 1) * 128], cat1[:, kk * T:(kk + 1) * T], kk == 0, kk == 15,
                             waits=([wtok, l2] + g) if kk == 0 else (), signal=(kk == 15))
                et = ev(u * 4 + c, bank[:], tok)
                banks.release(bi, et)
            wdone(s_, tok)
        cat_guard[j % 2] = pe_now()
        l2 = l2n
    barrier()

    mlp_phase(1, G_MLP1, x3T, (xmT, xnT), x4T)
    barrier()
    R.label = "G"
    XTS = [[hT[:].bitcast(F32), big[:].bitcast(F32)],
           [r_[:].bitcast(F32) for r_ in ring]]
    gsel = {"j": 0}

    def xch(c, n=1):
        if gsel["j"] % 2 == 0:
            return XTS[0][c // 16][:, (c % 16) * T:((c % 16) + n) * T]
        return XTS[1][c // 8][:, (c % 8) * T:((c % 8) + n) * T]

    og_bufs = [qm[:, 0:4 * T].bitcast(F32), qm[:, 4 * T:8 * T].bitcast(F32), stg[0][:], stg[1][:]]
    og_r = Rot(og_bufs)
    x_guard = [[("pe", R.cnt["pe"])], [("pe", R.cnt["pe"])]]
    R.wait("sp", [("w%d" % i_, R.dcnt["w%d" % i_]) for i_ in range(NSLOT)])

    def g_load(j):
        gsel["j"] = j
        lds_ = []
        for g4 in range(8):
            dstv = xch(g4 * 4, 4).rearrange("p (c t) -> p c t", t=T)
            ld = R.dma("sp", dstv, x4T[g4 * 512:(g4 + 1) * 512, j * T:(j + 1) * T].rearrange("(c p) t -> p c t", p=128),
                       f"xg{(j % 2) * 8 + g4}", waits=x_guard[j % 2] if g4 == 0 else [])
            lds_.append(ld)
        return lds_

    lds_next = g_load(0)
    for j in range(NT):
        c0 = j * T
        lds = lds_next
        if j + 1 < NT:
            lds_next = g_load(j + 1)
        gsel["j"] = j
        last = None
        for c2 in range(16):
            srcv = xch(c2 * 2, 2)
            if c2 == 0:
                last = act(acc[:, 0:2 * T], srcv, AF.Square, waits=[lds[0]] + ns["acc_g"])
            else:
                t1 = act(sq[:, 0:2 * T], srcv, AF.Square, waits=[lds[c2 // 2], last])
                last = tt(acc[:, 0:2 * T], acc[:, 0:2 * T], sq[:, 0:2 * T], ALU.add, waits=[t1, last])
        t2 = tt(acc1[:, 0:T], acc[:, 0:T], acc[:, T:2 * T], ALU.add, waits=[last] + ns["acc1_g"])
        ns["acc_g"] = [t2]
        rs = rms_stats_b(t2, T)
        rel = None
        for c2 in range(16):
            oi, ob, og = og_r.next()
            for cc in range(2):
                c = c2 * 2 + cc
                rel = stt(ob[:, cc * T:(cc + 1) * T], xch(c), gain_ap(G_FIN, c), rstd[:, 0:T], ALU.mult, ALU.mult,
                          waits=[rs, CST] + og)
            st = R.dma("sp", outT[c2 * 256:(c2 + 1) * 256, c0:c0 + T].rearrange("(c p) t -> p c t", p=128),
                       ob.rearrange("p (c t) -> p c t", t=T), f"og{oi}", waits=[rel])
            og_r.release(oi, st)
        x_guard[j % 2] = [rel]
        ns["rstd_g"].append(rel)
    R.wait("sp", sp_dma_toks())

    sem_keys = list(Rec.ENGS) + sorted(R.dcnt.keys())
    sems = {key: es.enter_context(nc.semaphore(f"s_{key}")) for key in sem_keys}
    block = es.enter_context(nc.Block())

    def emit(eng_name):
        def body(e):
            for item in R.streams[eng_name]:
                if item[0] == "wait":
                    e.wait_ge(sems[item[1]], item[2])
                else:
                    ins = item[1](e)
                    if item[2] is not None:
                        ins.then_inc(sems[item[2][0]], item[2][1])
        return body

    block.tensor(emit("pe"))
    block.scalar(emit("act"))
    block.vector(emit("dve"))
    block.gpsimd(emit("pool"))
    block.sync(emit("sp"))
    es.close()
    k.counts = {e: len(R.streams[e]) for e in Rec.ENGS}
    k.R = R
    k.sems = {key: (R.cnt.get(key) or R.dcnt.get(key)) for key in sem_keys}
    return nc, k


def _t5_bucket(dist):
    dist = np.asarray(dist, dtype=np.int32)
    d32 = np.maximum(dist, 1).astype(np.float32)
    large = 16 + (np.log(d32 / np.float32(16)) / np.float32(math.log(2048 / 16)) * np.float32(16)).astype(np.int32)
    large = np.minimum(large, 31)
    return np.where(dist < 16, dist, large)


def _bias_tiles(rel_bias):
    kj = np.arange(128)[:, None]
    qi = np.arange(128)[None, :]
    out = np.empty((128, 24, 2, 128), np.float32)
    for g, (win, dil) in enumerate(DIL):
        d_prev = qi + 128 - kj
        d_cur = qi - kj
        for half, delta in enumerate((d_prev, d_cur)):
            valid = (delta >= 0) & (delta <= 128)
            bucket = _t5_bucket(np.maximum(delta, 0) * dil)
            for h in range(8):
                col = rel_bias[:, g * 8 + h]
                tile = col[bucket]
                out[:, g * 8 + h, half, :] = np.where(valid, tile, np.float32(NEG))
    return np.ascontiguousarray(out.reshape(128, 24 * 256))


def _prep_shared(inp):
    f = lambda a: np.ascontiguousarray(np.asarray(a, dtype=np.float32))
    gl = [inp["a_norm"][0], inp["kv_norm"], inp["b_norm"][0], inp["mem_norm"], inp["mlp_norm"][0],
          inp["mlp_norm"][1], inp["final_norm"]]
    gains = np.stack([np.asarray(g, np.float32).reshape(NCH, 128).T for g in gl], axis=1)
    invc = np.ones((4, 16), np.float32)
    for g, w in enumerate((2, 4, 8, 16)):
        invc[g] = np.float32(1.0) / np.minimum(np.arange(16) + 1, w).astype(np.float32)
    shared = {
        "gains": f(gains.reshape(128, 7 * NCH)),
        "ascale": f(np.asarray(inp["a_scale"][0], np.float32).reshape(24, 128).T),
        "invcnt": f(np.broadcast_to(invc.reshape(1, 64), (128, 64))),
        "biasT": _bias_tiles(np.asarray(inp["rel_bias"], np.float32)),
        "a_w_in": f(inp["a_w_in"][0]),
        "a_w_pg": f(np.asarray(inp["a_w_pg"][0]).reshape(4 * 768, 768)),
        "a_w_out": f(inp["a_w_out"][0]),
        "w_kv": f(inp["w_kv"]),
        "b_w_in": f(inp["b_w_in"][0]),
        "b_w_out": f(inp["b_w_out"][0]),
        "w_mem_kv": f(np.asarray(inp["w_mem_kv"]).reshape(2 * D, 2048)),
        "mlp_w1": f(np.asarray(inp["mlp_w1"]).reshape(2 * D, DFF)),
        "mlp_w2": f(np.asarray(inp["mlp_w2"]).reshape(2 * DFF, D)),
    }
    return shared


_CACHE = {}


def kernel(**inputs):
    debug = bool(int(os.environ.get("YOCO_DEBUG", "0")))
    ncores = int(os.environ.get("YOCO_CORES", "8"))
    key = ("prog", debug)
    if key not in _CACHE:
        _CACHE[key] = build_program(debug=debug)
    nc, kinfo = _CACHE[key]
    shared = _prep_shared(inputs)
    x = np.asarray(inputs["x"], np.float32)
    mem = np.asarray(inputs["mem"], np.float32)
    in_maps = []
    for b in range(ncores):
        m = dict(shared)
        m["xT"] = np.ascontiguousarray(x[b].T)
        m["memT"] = np.ascontiguousarray(mem[b].T)
        in_maps.append(m)
    res = run_bass_kernel_spmd(nc, in_maps, core_ids=list(range(ncores)))
    if debug:
        _CACHE["last"] = res
    out = np.stack([np.ascontiguousarray(res.results[b]["outT"].T) for b in range(ncores)], axis=0)
    return out.astype(np.float32)
```

```python
import os
import math
import contextlib
import numpy as np
import concourse.bass as bass
import concourse.mybir as mybir
from concourse.bass_utils import run_bass_kernel_spmd

F32 = mybir.dt.float32
BF16 = mybir.dt.bfloat16
AF = mybir.ActivationFunctionType
ALU = mybir.AluOpType

D = 4096
S = 2048
NCH = 32
T = 512
NT = S // T
MEM = 256
DFF = 16384
EPS = 1e-6
NEG = -30000.0
NSLOT = 4
SLOT_ELEMS = 8192
DIL = ((128, 1), (512, 4), (2048, 16))

G_A, G_KV, G_B, G_MEM, G_MLP0, G_MLP1, G_FIN = range(7)


class Rec:
    ENGS = ("pe", "act", "dve", "pool", "sp")

    def __init__(self):
        self.streams = {e: [] for e in self.ENGS}
        self.cnt = {e: 0 for e in self.ENGS}
        self.seen = {e: {} for e in self.ENGS}
        self.dcnt = {}
        self.label = ""

    def _flat(self, waits, out):
        for tok in waits:
            if tok is None:
                continue
            if isinstance(tok, list):
                self._flat(tok, out)
            else:
                key, val = tok
                if val > out.get(key, 0):
                    out[key] = val
        return out

    def _waits(self, eng, waits):
        st = self.streams[eng]
        seen = self.seen[eng]
        for key, val in self._flat(waits, {}).items():
            if key == "pe" and eng == "pe":
                continue
            if seen.get(key, 0) >= val:
                continue
            seen[key] = val
            st.append(("wait", key, val, self.label))

    def op(self, eng, fn, waits=(), signal=True):
        self._waits(eng, waits)
        if signal:
            self.cnt[eng] += 1
            tok = (eng, self.cnt[eng])
            self.streams[eng].append(("op", fn, (eng, 1)))
            return tok
        self.streams[eng].append(("op", fn, None))
        return None

    def dma(self, eng, out, in_, dsem, waits=()):
        self._waits(eng, waits)
        self.dcnt[dsem] = self.dcnt.get(dsem, 0) + 16
        self.streams[eng].append(("op", lambda e, o=out, i=in_: e.dma_start(out=o, in_=i), (dsem, 16)))
        return (dsem, self.dcnt[dsem])

    def wait(self, eng, waits):
        self._waits(eng, waits)


class K:
    pass


def build_program(debug=False):
    nc = bass.Bass("TRN2", target_bir_lowering=False)
    R = Rec()
    k = K()

    def din(name, shape, dt=F32):
        return nc.dram_tensor(name, list(shape), dt, kind="ExternalInput").ap()

    def dscr(name, shape, dt):
        kind = "ExternalOutput" if debug else "Internal"
        return nc.dram_tensor(name, list(shape), dt, kind=kind).ap()

    xT = din("xT", [D, S])
    memT = din("memT", [D, MEM])
    gains = din("gains", [128, 7 * NCH])
    ascale = din("ascale", [128, 24])
    invcnt = din("invcnt", [128, 4 * 16])
    biasT = din("biasT", [128, 24 * 256])
    a_w_in = din("a_w_in", [D, D])
    a_w_pg = din("a_w_pg", [4 * 768, 768])
    a_w_out = din("a_w_out", [D, D])
    w_kv = din("w_kv", [D, 6144])
    b_w_in = din("b_w_in", [D, D])
    b_w_out = din("b_w_out", [2048, D])
    w_mem_kv = din("w_mem_kv", [2 * D, 2048])
    mlp_w1 = din("mlp_w1", [2 * D, DFF])
    mlp_w2 = din("mlp_w2", [2 * DFF, D])
    outT = nc.dram_tensor("outT", [D, S], F32, kind="ExternalOutput").ap()

    x1T = dscr("x1T", [D, S], F32)
    xmT = dscr("xmT", [D, S], F32)
    x2T = dscr("x2T", [D, S], F32)
    x3T = dscr("x3T", [D, S], F32)
    xnT = dscr("xnT", [D, S], F32)
    x4T = dscr("x4T", [D, S], F32)
    KTd = dscr("KTd", [3072, S], BF16)
    Vd = dscr("Vd", [S, 3072], BF16)
    QTd = dscr("QTd", [3072, S], BF16)
    MO1 = dscr("MO1", [1024, S], BF16)
    ATT = dscr("ATT", [1024, S], BF16)

    es = contextlib.ExitStack()

    def sb(name, shape, dt):
        return es.enter_context(nc.sbuf_tensor(name, list(shape), dt))

    ring = [sb(f"ring{i}", [128, SLOT_ELEMS], BF16) for i in range(NSLOT)]
    hT = sb("hT", [128, NCH * T], BF16)
    big = sb("big", [128, 16384], BF16)
    qm = sb("qm", [128, 8 * T], BF16)
    stg = [sb(f"stg{i}", [128, 2 * T], F32) for i in range(2)]
    sq = sb("sq", [128, 2 * T], F32)
    acc = sb("acc", [128, 2 * T], F32)
    acc1 = sb("acc1", [128, T], F32)
    rt = sb("rt", [128, T], F32)
    rstd = sb("rstd", [128, T], F32)
    mkT = sb("mkT", [128, 2 * 8 * MEM], BF16)
    mv = sb("mv", [128, 2 * 2 * 1024], BF16)
    gains_sb = sb("gains_sb", [128, 7 * NCH], F32)
    ascale_sb = sb("ascale_sb", [128, 24], F32)
    invcnt_sb = sb("invcnt_sb", [128, 64], F32)
    ones_bf = sb("ones_bf", [128, 128], BF16)
    ones_f = sb("ones_f", [128, 128], F32)
    xst = [sb(f"xst{i}", [128, T], F32) for i in range(2)]
    ost = [sb(f"ost{i}", [128, T], F32) for i in range(2)]
    bst = [sb(f"bst{i}", [128, T], BF16) for i in range(3)]
    ubuf = [sb(f"ubuf{i}", [128, 16 + T], F32) for i in range(2)]
    sA = sb("sA", [128, 16 + T], F32)
    sB = sb("sB", [128, 16 + T], F32)
    halo = sb("halo", [128, 24 * 16], F32)
    fix16 = sb("fix16", [128, 16], F32)
    ftmp = [ubuf[0][:, 0:T], ubuf[1][:, 0:T]]
    xstx = [sb(f"xstx{i}", [128, T], F32) for i in range(2)]
    pt_sb = [sb(f"pt{i}", [128, 2 * T], BF16) for i in range(2)]
    rden = [sb(f"rden{i}", [128, T], F32) for i in range(1)]

    psum = [es.enter_context(nc.psum_tensor(f"ps{i}", [128, T], F32)) for i in range(8)]

    class Rot:
        def __init__(self, bufs):
            self.bufs = bufs
            self.i = 0
            self.guard = [[] for _ in bufs]

        def next(self):
            i = self.i
            self.i = (self.i + 1) % len(self.bufs)
            g = self.guard[i]
            self.guard[i] = []
            return i, self.bufs[i], g

        def release(self, i, tok):
            self.guard[i].append(tok)

    banks = Rot(psum[:7])
    STAT = psum[7]
    ost_r = Rot(ost)
    bst_r = Rot(bst)
    ftmp_r = Rot(ftmp)
    stg_r = Rot(stg)
    ubuf_r = Rot(ubuf)
    pt_r = Rot(pt_sb)
    rden_r = Rot(rden)

    def sp_dma_toks():
        return [(key, v) for key, v in R.dcnt.items() if not key.startswith("w")]

    def barrier():
        toks = [(e, R.cnt[e]) for e in ("pe", "act", "dve") if R.cnt[e] > 0] + sp_dma_toks()
        for e in ("pe", "act", "dve", "sp"):
            R.wait(e, [t for t in toks if t[0] != e])

    wstate = {"n": 0, "free": [[] for _ in range(NSLOT)]}

    def wget(src_ap, kc, cols):
        assert kc * cols <= SLOT_ELEMS
        s_ = wstate["n"] % NSLOT
        wstate["n"] += 1
        view = ring[s_][:, 0:kc * cols].rearrange("p (k c) -> p k c", c=cols)
        g = wstate["free"][s_]
        wstate["free"][s_] = []
        tok = R.dma("pool", view, src_ap, f"w{s_}", waits=g)
        return s_, view, tok

    def wdone(s_, tok):
        wstate["free"][s_].append(tok)

    def wsrc(w, r0, kc, c0, cols):
        return w[r0:r0 + kc * 128, c0:c0 + cols].rearrange("(k p) c -> p k c", p=128)

    def mm(out, lhsT, rhs, start, stop, waits=(), signal=False):
        return R.op("pe", lambda e: e.matmul(out, lhsT, rhs, start=start, stop=stop), waits=waits, signal=signal)

    def act(out, in_, func, waits=(), scale=1.0, bias=None):
        if bias is None:
            return R.op("act", lambda e: e.activation(out=out, in_=in_, func=func, scale=scale), waits=waits)
        return R.op("act", lambda e: e.activation(out=out, in_=in_, func=func, bias=bias, scale=scale), waits=waits)

    def tt(out, in0, in1, op, waits=(), eng="dve"):
        return R.op(eng, lambda e: e.tensor_tensor(out=out, in0=in0, in1=in1, op=op), waits=waits)

    def stt(out, in0, scalar, in1, op0, op1, waits=(), eng="dve"):
        return R.op(eng, lambda e: e.scalar_tensor_tensor(out=out, in0=in0, scalar=scalar, in1=in1, op0=op0, op1=op1),
                    waits=waits)

    def cp(out, in_, waits=(), eng="dve"):
        return R.op(eng, lambda e: e.tensor_copy(out=out, in_=in_), waits=waits)

    def recip(out, in_, waits=()):
        return R.op("dve", lambda e: e.reciprocal(out=out, in_=in_), waits=waits)

    def memset(ap, val, waits=()):
        return R.op("dve", lambda e: e.memset(ap, val), waits=waits)

    R.dma("sp", gains_sb[:], gains[:], "cst")
    R.dma("sp", ascale_sb[:], ascale[:], "cst")
    CST = R.dma("sp", invcnt_sb[:], invcnt[:], "cst")
    eps_sb = sb("eps_sb", [128, 1], F32)
    ONES = [memset(ones_f[:], 1.0), memset(ones_bf[:], 1.0), memset(halo[:], 0.0), memset(eps_sb[:], EPS)]

    def gain_ap(gi, c):
        return gains_sb[:, gi * NCH + c: gi * NCH + c + 1]

    ns = {"acc_g": [], "acc1_g": [], "rt_g": [], "stat_g": [], "rstd_g": []}

    def rms_stats(src, c0, n, split=False):
        last = None
        for cg in range(NCH // 2):
            si, sbuf_, g = stg_r.next()
            sv = sbuf_[:, 0:2 * n].rearrange("p (c t) -> p c t", t=n)
            ld = R.dma("sp", sv, src[cg * 256:(cg + 1) * 256, c0:c0 + n].rearrange("(c p) t -> p c t", p=128),
                       f"stg{si}", waits=g)
            if cg == 0:
                t1 = act(acc[:, 0:2 * n], sbuf_[:, 0:2 * n], AF.Square, waits=[ld] + ns["acc_g"])
                stg_r.release(si, t1)
                last = t1
            else:
                t1 = act(sq[:, 0:2 * n], sbuf_[:, 0:2 * n], AF.Square, waits=[ld, last])
                stg_r.release(si, t1)
                last = tt(acc[:, 0:2 * n], acc[:, 0:2 * n], sq[:, 0:2 * n], ALU.add, waits=[t1, last])
        t2 = tt(acc1[:, 0:n], acc[:, 0:n], acc[:, n:2 * n], ALU.add, waits=[last] + ONES + ns["acc1_g"])
        ns["acc_g"] = [t2]
        if split:
            return t2
        return rms_stats_b(t2, n)

    def rms_stats_b(t2, n):
        t3 = mm(STAT[:, 0:n], ones_f[:], acc1[:, 0:n], True, True, waits=[t2] + ns["stat_g"], signal=True)
        ns["acc1_g"] = [t3]
        t4 = act(rt[:, 0:n], STAT[:, 0:n], AF.Sqrt, waits=[t3] + ns["rt_g"], scale=1.0 / D, bias=eps_sb[:, 0:1])
        ns["stat_g"] = [t4]
        t5 = recip(rstd[:, 0:n], rt[:, 0:n], waits=[t4] + ns["rstd_g"])
        ns["rt_g"] = [t5]
        ns["rstd_g"] = []
        return t5

    def rms_apply(src, c0, n, rstd_tok, outs, h_guard):
        tk = None
        for cg in range(NCH // 2):
            si, sbuf_, g = stg_r.next()
            sv = sbuf_[:, 0:2 * n].rearrange("p (c t) -> p c t", t=n)
            ld = R.dma("sp", sv, src[cg * 256:(cg + 1) * 256, c0:c0 + n].rearrange("(c p) t -> p c t", p=128),
                       f"stg{si}", waits=g)
            for cc in range(2):
                c = cg * 2 + cc
                for (gi, dst) in outs:
                    tk = stt(dst(c), sbuf_[:, cc * n:(cc + 1) * n], gain_ap(gi, c), rstd[:, 0:n], ALU.mult, ALU.mult,
                             waits=[ld, rstd_tok, CST] + h_guard)
            stg_r.release(si, tk)
        ns["rstd_g"].append(tk)
        return tk

    def _stg_load(src, c0, n, cg):
        si, sbuf_, g = stg_r.next()
        sv = sbuf_[:, 0:2 * n].rearrange("p (c t) -> p c t", t=n)
        ld = R.dma("sp", sv, src[cg * 256:(cg + 1) * 256, c0:c0 + n].rearrange("(c p) t -> p c t", p=128),
                   f"stg{si}", waits=g)
        return si, sbuf_, ld

    def rms_stats_gen(src, c0, n, out):
        last = None
        nxt = _stg_load(src, c0, n, 0)
        for cg in range(NCH // 2):
            si, sbuf_, ld = nxt
            if cg == 0:
                t1 = act(acc[:, 0:2 * n], sbuf_[:, 0:2 * n], AF.Square, waits=[ld] + ns["acc_g"])
                stg_r.release(si, t1)
                last = t1
            else:
                t1 = act(sq[:, 0:2 * n], sbuf_[:, 0:2 * n], AF.Square, waits=[ld, last])
                stg_r.release(si, t1)
                last = tt(acc[:, 0:2 * n], acc[:, 0:2 * n], sq[:, 0:2 * n], ALU.add, waits=[t1, last])
            if cg + 1 < NCH // 2:
                nxt = _stg_load(src, c0, n, cg + 1)
            else:
                t2 = tt(acc1[:, 0:n], acc[:, 0:n], acc[:, n:2 * n], ALU.add, waits=[last] + ONES + ns["acc1_g"])
                ns["acc_g"] = [t2]
                out["t2"] = t2
            yield

    def rms_apply_gen(src, c0, n, rstd_tok, gi, dst, h_guard, out):
        tk = None
        nxt = _stg_load(src, c0, n, 0)
        for cg in range(NCH // 2):
            si, sbuf_, ld = nxt
            if cg + 1 < NCH // 2:
                nxt = _stg_load(src, c0, n, cg + 1)
            for cc in range(2):
                c = cg * 2 + cc
                tk = stt(dst(c), sbuf_[:, cc * n:(cc + 1) * n], gain_ap(gi, c), rstd[:, 0:n], ALU.mult, ALU.mult,
                         waits=[ld, rstd_tok, CST] + h_guard)
            stg_r.release(si, tk)
            out["tok"] = tk
            yield
        ns["rstd_g"].append(tk)

    def proj_A(w, r0, nk, c0, ncols, rhs_fn, rhs_tok, n, evac, cols=512, kcu=16):
        assert kcu * cols <= SLOT_ELEMS and nk % kcu == 0 and ncols % cols == 0
        mpu = cols // 128
        nu = nk // kcu
        for gidx in range(ncols // cols):
            bb = [banks.next() for _ in range(mpu)]
            allg = [b_[2] for b_ in bb]
            for u in range(nu):
                s_, wv, wtok = wget(wsrc(w, r0 + u * kcu * 128, kcu, c0 + gidx * cols, cols), kcu, cols)
                tok = None
                for c in range(mpu):
                    bi, bank, g = bb[c]
                    for kk in range(kcu):
                        kabs = u * kcu + kk
                        tok = mm(bank[:, 0:n], wv[:, kk, c * 128:(c + 1) * 128], rhs_fn(kabs), kabs == 0, kabs == nk - 1,
                                 waits=([wtok, rhs_tok] + (allg if u == 0 else [])) if kk == 0 else (),
                                 signal=(kk == kcu - 1))
                    if u == nu - 1:
                        et = evac(gidx * mpu + c, bank[:, 0:n], tok)
                        banks.release(bi, et)
                wdone(s_, tok)

    def mem_attention(l, q_fn, q_tok, dst_fn, after):
        for h in range(4):
            pi, pt, pg = pt_r.next()
            etoks = []
            for mch in range(2):
                bi, bank, g = banks.next()
                tok = None
                for cc in range(2):
                    base = (l * 8 + h * 2 + cc) * MEM + mch * 128
                    tok = mm(bank[:], mkT[:, base:base + 128], q_fn(h * 2 + cc), cc == 0, cc == 1,
                             waits=([q_tok, MEMTOK] + g) if cc == 0 else (), signal=(cc == 1))
                et = act(pt[:, mch * T:(mch + 1) * T], bank[:], AF.Exp, waits=[tok] + pg, scale=1.0 / 16.0)
                banks.release(bi, et)
                etoks.append(et)
            bi, bank, g = banks.next()
            tok = None
            for mch in range(2):
                tok = mm(bank[:], ones_bf[:], pt[:, mch * T:(mch + 1) * T], mch == 0, mch == 1,
                         waits=(etoks + g + ONES) if mch == 0 else (), signal=(mch == 1))
            ri, rd, rg = rden_r.next()
            rtok = recip(rd[:], bank[:], waits=[tok] + rg)
            banks.release(bi, rtok)
            last = None
            for oc in range(2):
                bi, bank, g = banks.next()
                tok = None
                for mch in range(2):
                    base = (l * 2 + mch) * 1024 + h * 256 + oc * 128
                    tok = mm(bank[:], mv[:, base:base + 128], pt[:, mch * T:(mch + 1) * T], mch == 0, mch == 1,
                             waits=(etoks + g) if mch == 0 else (), signal=(mch == 1))
                dst, dg = dst_fn(h * 2 + oc)
                ot = tt(dst, bank[:], rd[:], ALU.mult, waits=[tok, rtok] + dg)
                banks.release(bi, ot)
                after(h * 2 + oc, ot)
                last = ot
            rden_r.release(ri, last)
            pt_r.release(pi, tok)

    xs = {"r": Rot([b_[:] for b_ in xst])}

    def resid_evac(src, dst, c0, nchunks=NCH):
        xr = xs["r"]
        depth = len(xr.bufs) - 1
        pending = {}

        def issue(m):
            xi, xb, xg = xr.next()
            ld = R.dma("sp", xb, src[m * 128:(m + 1) * 128, c0:c0 + T], f"xst{xi}", waits=xg)
            pending[m] = (xi, xb, ld)

        for m in range(min(depth, nchunks)):
            issue(m)

        def evac(m, ps, tok):
            xi, xb, ld = pending.pop(m)
            oi, ob, og = ost_r.next()
            et = tt(ob[:], ps, xb, ALU.add, waits=[tok, ld] + og)
            xr.release(xi, et)
            if m + depth < nchunks:
                issue(m + depth)
            st = R.dma("sp", dst[m * 128:(m + 1) * 128, c0:c0 + T], ob[:], f"ost{oi}", waits=[et])
            ost_r.release(oi, st)
            return et
        return evac

    def store_bf(dst_ap, make):
        i, b, g = bst_r.next()
        et = make(b[:], g)
        st = R.dma("sp", dst_ap, b[:], f"bst{i}", waits=[et])
        bst_r.release(i, st)
        return et

    def pe_now():
        return [("pe", R.cnt["pe"])]

    R.label = "M"
    hmem = big[:, 0:NCH * MEM]
    rs = rms_stats(memT, 0, MEM)
    hm_tok = rms_apply(memT, 0, MEM, rs, [(G_MEM, lambda c: hmem[:, c * MEM:(c + 1) * MEM])], [])
    for l in range(2):
        def ev_k(m, ps, tok, l=l):
            return act(mkT[:, (l * 8 + m) * MEM:(l * 8 + m + 1) * MEM], ps, AF.Copy, waits=[tok])
        proj_A(w_mem_kv, l * D, NCH, 0, 1024, lambda kk: hmem[:, kk * MEM:(kk + 1) * MEM], hm_tok, MEM, ev_k)
        for cg in range(2):
            bb = [banks.next() for _ in range(2)]
            tok = None
            for u in range(2):
                s_, wv, wtok = wget(wsrc(w_mem_kv, l * D + u * 2048, 16, 1024 + cg * 512, 512), 16, 512)
                for mch in range(2):
                    bi, bank, g = bb[mch]
                    for kk in range(16):
                        kabs = u * 16 + kk
                        tok = mm(bank[:], hmem[:, kabs * MEM + mch * 128: kabs * MEM + mch * 128 + 128], wv[:, kk, :],
                                 kabs == 0, kabs == NCH - 1,
                                 waits=([wtok, hm_tok] + (g if u == 0 else [])) if kk == 0 else (),
                                 signal=(kk == 15))
                wdone(s_, tok)
            for mch in range(2):
                bi, bank, g = bb[mch]
                base = (l * 2 + mch) * 1024 + cg * 512
                et = act(mv[:, base:base + 512], bank[:], AF.Copy, waits=[tok])
                banks.release(bi, et)
    MEMTOK = ("act", R.cnt["act"])
    barrier()

    pooledT = big[:, 0:24 * T]
    concat = hT
    qmem = qm
    h_guard = []
    halo_tok = {}
    t2n = rms_stats(xT, 0, T, split=True)
    for j in range(NT):
        c0 = j * T
        R.label = "A.norm"
        rs = rms_stats_b(t2n, T)
        h_tok = rms_apply(xT, c0, T, rs, [(G_A, lambda c: hT[:, c * T:(c + 1) * T])], h_guard)
        state = {"pooled_last": None, "q_last": None, "big_guard": list(h_guard), "s_guard": []}

        def ev_in(m, ps, tok, j=j, state=state):
            if m >= 24:
                t_ = act(qmem[:, (m - 24) * T:(m - 23) * T], ps, AF.Copy, waits=[tok] + state["big_guard"])
                state["q_last"] = t_
                return t_
            g_ = m // 6
            L = g_ + 1
            w_ = 2 ** L
            ui, ub, ug = ubuf_r.next()
            t_ = act(ub[:, 16:16 + T], ps, AF.Copy, waits=[tok] + ug)
            if j == 0:
                th = memset(ub[:, 0:16], 0.0, waits=ug)
            else:
                th = cp(ub[:, 0:16], halo[:, m * 16:(m + 1) * 16], waits=ug + [halo_tok[m]])
            src_ = ub
            pp = [sA, sB]
            lastt = [t_, th]
            for lv in range(1, L + 1):
                sh = 2 ** (lv - 1)
                lo = 2 ** lv - 1
                dst_ = pp[(lv - 1) % 2]
                tl = tt(dst_[:, lo:16 + T], src_[:, lo:16 + T], src_[:, lo - sh:16 + T - sh], ALU.add,
                        waits=lastt + state["s_guard"])
                lastt = [tl]
                src_ = dst_
            tf = stt(pooledT[:, m * T:(m + 1) * T], src_[:, 16:16 + T], 1.0 / w_, ub[:, 16:16 + T], ALU.mult,
                     ALU.subtract, waits=lastt + state["big_guard"])
            if j == 0:
                tf1 = tt(fix16[:, 0:16], src_[:, 16:32], invcnt_sb[:, g_ * 16:(g_ + 1) * 16], ALU.mult,
                         waits=[tf, CST])
                tf = tt(pooledT[:, m * T:m * T + 16], fix16[:, 0:16], ub[:, 16:32], ALU.subtract, waits=[tf1])
            th2 = cp(halo[:, m * 16:(m + 1) * 16], ub[:, T:T + 16], waits=[tf])
            halo_tok[m] = th2
            ubuf_r.release(ui, th2)
            state["s_guard"] = [tf]
            state["pooled_last"] = th2
            return t_

        R.label = "A.in"
        proj_A(a_w_in, 0, NCH, 0, D, lambda kk: hT[:, kk * T:(kk + 1) * T], h_tok, T, ev_in)
        cc_tok = {}
        R.label = "A.pg"
        for g_ in range(4):
            s_, wv, wtok = wget(wsrc(a_w_pg, g_ * 768, 6, 0, 768), 6, 768)
            tok = None
            for dc in range(6):
                bi, bank, bg = banks.next()
                for cc in range(6):
                    tok = mm(bank[:], wv[:, cc, dc * 128:(dc + 1) * 128],
                             pooledT[:, (g_ * 6 + cc) * T:(g_ * 6 + cc + 1) * T],
                             cc == 0, cc == 5, waits=([wtok, state["pooled_last"]] + bg) if cc == 0 else (),
                             signal=(cc == 5))
                m = g_ * 6 + dc
                et = act(concat[:, m * T:(m + 1) * T], bank[:], AF.Copy, waits=[tok, CST],
                         scale=ascale_sb[:, m:m + 1])
                banks.release(bi, et)
                cc_tok[m] = et
            wdone(s_, tok)
        R.label = "A.mem"
        mem_attention(0, lambda c: qmem[:, c * T:(c + 1) * T], state["q_last"],
                      lambda c: (concat[:, (24 + c) * T:(25 + c) * T], []),
                      lambda c, tok: cc_tok.__setitem__(24 + c, tok))
        cat_tok = [cc_tok[m] for m in range(32)]
        if j + 1 < NT:
            R.label = "A.norm"
            t2n = rms_stats(xT, c0 + T, T, split=True)
        R.label = "A.out"
        proj_A(a_w_out, 0, NCH, 0, D, lambda kk: concat[:, kk * T:(kk + 1) * T], cat_tok, T,
               resid_evac(xT, x1T, c0))
        h_guard = pe_now()
    barrier()

    xst_deep_aps = [xst[0][:], xst[1][:], sA[:, 0:T], sB[:, 0:T], xstx[0][:], xstx[1][:]]
    FFG = 32
    NQ = DFF // (FFG * 128)

    def mlp_phase(l, gi, x_in, mids, x_out):
        aT = big
        nstate = {}
        xs["r"] = Rot(list(xst_deep_aps))
        a_guard = []
        rs = rms_stats(x_in, 0, T)
        h_tok = rms_apply(x_in, 0, T, rs, [(gi, lambda c: hT[:, c * T:(c + 1) * T])], [])
        for j in range(NT):
            c0 = j * T
            h_next = None
            for q in range(NQ):
                a_last = {}
                ag = list(a_guard)

                def ev1(m, ps, tok, ag=ag, a_last=a_last):
                    fi, fb, fg = ftmp_r.next()
                    t1 = act(fb, ps, AF.Relu, waits=[tok] + fg)
                    t2 = tt(aT[:, m * T:(m + 1) * T], fb, fb, ALU.mult, waits=[t1] + ag)
                    ftmp_r.release(fi, t2)
                    a_last["t"] = t2
                    return t1

                R.label = f"mlp{l}.up"
                proj_A(mlp_w1, l * D, NCH, q * FFG * 128, FFG * 128, lambda kk: hT[:, kk * T:(kk + 1) * T], h_tok, T, ev1)
                a_tok = a_last["t"]
                R.label = f"mlp{l}.norm"
                side = None
                sout = {}
                if q == NQ - 2 and j + 1 < NT:
                    side = rms_stats_gen(x_in, c0 + T, T, nstate)
                if q == NQ - 1 and j + 1 < NT:
                    rs = rms_stats_b(nstate["t2"], T)
                    side = rms_apply_gen(x_in, c0 + T, T, rs, gi, lambda c: hT[:, c * T:(c + 1) * T], pe_now(), sout)
                src = x_in if q == 0 else mids[(q - 1) % 2]
                dst = x_out if q == NQ - 1 else mids[q % 2]
                if q > 0:
                    R.wait("sp", sp_dma_toks())
                R.label = f"mlp{l}.down"
                ev2 = resid_evac(src, dst, c0)
                for dg in range(8):
                    bb = [banks.next() for _ in range(4)]
                    allg = [b_[2] for b_ in bb]
                    for u in range(2):
                        s_, wv, wtok = wget(wsrc(mlp_w2, l * DFF + q * FFG * 128 + u * 2048, 16, dg * 512, 512), 16, 512)
                        tok = None
                        for c in range(4):
                            bi, bank, g = bb[c]
                            for kk in range(16):
                                kabs = u * 16 + kk
                                tok = mm(bank[:], wv[:, kk, c * 128:(c + 1) * 128], aT[:, kabs * T:(kabs + 1) * T],
                                         kabs == 0, kabs == FFG - 1,
                                         waits=([wtok, a_tok] + (allg if u == 0 else [])) if kk == 0 else (),
                                         signal=(kk == 15))
                            if u == 1:
                                et = ev2(dg * 4 + c, bank[:], tok)
                                banks.release(bi, et)
                                if side is not None:
                                    next(side, None)
                        wdone(s_, tok)
                if side is not None:
                    for _ in side:
                        pass
                if q == NQ - 1 and j + 1 < NT:
                    h_next = sout["tok"]
                a_guard = pe_now()
            h_tok = h_next

    mlp_phase(0, G_MLP0, x1T, (xmT, xnT), x2T)
    barrier()

    hT2 = big
    qmem1 = qm
    rs = rms_stats(x2T, 0, T)
    hk_tok = rms_apply(x2T, 0, T, rs, [(G_KV, lambda c: hT[:, c * T:(c + 1) * T])], [])
    hb_tok = rms_apply(x2T, 0, T, rs, [(G_B, lambda c: hT2[:, c * T:(c + 1) * T])], [])
    for j in range(NT):
        c0 = j * T
        R.label = "C"
        h_tok = [hk_tok, hb_tok]

        def ev_kT(m, ps, tok, c0=c0):
            return store_bf(KTd[m * 128:(m + 1) * 128, c0:c0 + T],
                            lambda b, g: act(b, ps, AF.Copy, waits=[tok] + g))

        proj_A(w_kv, 0, NCH, 0, 3072, lambda kk: hT[:, kk * T:(kk + 1) * T], hk_tok, T, ev_kT)
        for cg in range(6):
            bb = [banks.next() for _ in range(4)]
            tok = None
            for u in range(2):
                s_, wv, wtok = wget(wsrc(w_kv, u * 2048, 16, 3072 + cg * 512, 512), 16, 512)
                for tc in range(4):
                    bi, bank, g = bb[tc]
                    for kk in range(16):
                        kabs = u * 16 + kk
                        tok = mm(bank[:], hT[:, kabs * T + tc * 128: kabs * T + tc * 128 + 128], wv[:, kk, :],
                                 kabs == 0, kabs == NCH - 1,
                                 waits=([wtok, hk_tok] + (g if u == 0 else [])) if kk == 0 else (),
                                 signal=(kk == 15))
                wdone(s_, tok)
            for tc in range(4):
                bi, bank, g = bb[tc]
                et = store_bf(Vd[c0 + tc * 128:c0 + (tc + 1) * 128, cg * 512:(cg + 1) * 512],
                              lambda b, gg, bank=bank, tok=tok: act(b, bank[:], AF.Copy, waits=[tok] + gg))
                banks.release(bi, et)
        if j + 1 < NT:
            rs = rms_stats(x2T, c0 + T, T)
            hk_tok = rms_apply(x2T, c0 + T, T, rs, [(G_KV, lambda c: hT[:, c * T:(c + 1) * T])], pe_now())
        qstate = {"q_last": None}

        def ev_q(m, ps, tok, c0=c0, qstate=qstate):
            if m >= 24:
                t_ = act(qmem1[:, (m - 24) * T:(m - 23) * T], ps, AF.Copy, waits=[tok])
                qstate["q_last"] = t_
                return t_
            return store_bf(QTd[m * 128:(m + 1) * 128, c0:c0 + T],
                            lambda b, g: act(b, ps, AF.Copy, waits=[tok] + g))

        proj_A(b_w_in, 0, NCH, 0, D, lambda kk: hT2[:, kk * T:(kk + 1) * T], hb_tok, T, ev_q)
        hb_guard = pe_now()
        pend = {}

        def mo_dst(c, pend=pend):
            i, b, g = bst_r.next()
            pend[c] = (i, b)
            return b[:], g

        def mo_after(c, tok, pend=pend, c0=c0):
            i, b = pend[c]
            st = R.dma("sp", MO1[c * 128:(c + 1) * 128, c0:c0 + T], b[:], f"bst{i}", waits=[tok])
            bst_r.release(i, st)

        mem_attention(1, lambda c: qmem1[:, c * T:(c + 1) * T], qstate["q_last"], mo_dst, mo_after)
        if j + 1 < NT:
            hb_tok = rms_apply(x2T, c0 + T, T, rs, [(G_B, lambda c: hT2[:, c * T:(c + 1) * T])], hb_guard)
    barrier()

    R.label = "D"
    qkv = []
    bt_t = []
    for i in range(2):
        o = i * 6656
        qkv.append((big[:, o:o + 2048], big[:, o + 2048:o + 4096], big[:, o + 4096:o + 6144]))
        bt_t.append(big[:, o + 6144:o + 6656].bitcast(F32))
    accn = hT[:, 0:4096].bitcast(F32)
    accd = hT[:, 4096:8192].bitcast(F32)
    tmpS = [hT[:, 8192 + i * 1024: 8192 + (i + 1) * 1024].bitcast(F32) for i in range(2)]
    ptd = [hT[:, 10240 + i * 512: 10240 + (i + 1) * 512] for i in range(2)]
    oh = [hT[:, 11264 + i * 2048: 11264 + (i + 1) * 2048] for i in range(2)]
    qkv_r = Rot([(q_[0], q_[1], q_[2], bt_t[i]) for i, q_ in enumerate(qkv)])
    oh_r = Rot(oh)
    SC = 1.0 / math.sqrt(128.0)
    tmpS_r = Rot(tmpS + [sq[:, 0:T], sq[:, T:2 * T]])
    ptd_r = Rot(ptd + [qm[:, 0:T], qm[:, T:2 * T]])

    steps_all = []
    for h in range(8):
        for g_, (win, d) in enumerate(DIL):
            nblk = (S // d) // 128
            full = [(r, n) for r in range(d) for n in range(1, nblk)]
            first = [(r, 0) for r in range(d)]
            st_list = [("full", full[i:i + 2]) for i in range(0, len(full), 2)] + \
                      [("first", first[i:i + 4]) for i in range(0, len(first), 4)]
            for si_, (kind, blks) in enumerate(st_list):
                steps_all.append(dict(h=h, g=g_, d=d, nblk=nblk, kind=kind, blks=blks, first_gh=(si_ == 0),
                                      last_gh=(si_ == len(st_list) - 1), last_head=(g_ == 2 and si_ == len(st_list) - 1)))

    cur = {}
    dstate = {"acc_last": None, "acc_guard": []}

    def tokslice(r, n, d):
        st_ = r + d * 128 * n
        return slice(st_, st_ + 127 * d + 1, d)

    def stage1(sp_):
        h, g_, d, nblk, kind, blks = sp_["h"], sp_["g"], sp_["d"], sp_["nblk"], sp_["kind"], sp_["blks"]
        gh = g_ * 8 + h
        if sp_["first_gh"]:
            qi_, (qh, kh, vh, bth), qg = qkv_r.next()
            R.dma("sp", qh, QTd[gh * 128:(gh + 1) * 128, :], f"qkv{qi_}", waits=qg)
            R.dma("sp", kh, KTd[gh * 128:(gh + 1) * 128, :], f"qkv{qi_}")
            ld = R.dma("sp", bth, biasT[:, gh * 256:(gh + 1) * 256], f"qkv{qi_}")
            vsrc = Vd[:, gh * 128:(gh + 1) * 128].rearrange("(n i r) c -> i r n c", i=128, r=d)
            vdst = vh.rearrange("p (r n c) -> p r n c", r=d, n=nblk)
            for r in range(d):
                ld = R.dma("sp", vdst[:, r], vsrc[:, r], f"qkv{qi_}")
            cur.update(qi=qi_, qh=qh, kh=kh, vh=vh, bth=bth, ld=ld)
        sp_.update(qi=cur["qi"], qh=cur["qh"], kh=cur["kh"], vh=cur["vh"], bth=cur["bth"], ld=cur["ld"])
        qh, kh, bth, ld = sp_["qh"], sp_["kh"], sp_["bth"], sp_["ld"]
        nh = 2 if kind == "full" else 1
        wdt = nh * 128
        bi, bank, bg = banks.next()
        tok = None
        first_mm = True
        for b_, (r, n) in enumerate(blks):
            for hf in range(nh):
                kn = (n - 1 + hf) if kind == "full" else n
                tok = mm(bank[:, b_ * wdt + hf * 128: b_ * wdt + hf * 128 + 128],
                         kh[:, tokslice(r, kn, d)], qh[:, tokslice(r, n, d)], True, True,
                         waits=([ld] + bg) if first_mm else (),
                         signal=(b_ == len(blks) - 1 and hf == nh - 1))
                first_mm = False
        W_ = len(blks) * wdt
        ti, tS, tg = tmpS_r.next()
        tl = None
        boff = 0 if kind == "full" else 128
        for b_ in range(len(blks)):
            tl = stt(tS[:, b_ * wdt:(b_ + 1) * wdt], bank[:, b_ * wdt:(b_ + 1) * wdt], SC,
                     bth[:, boff:boff + wdt], ALU.mult, ALU.add, waits=[tok, ld] + tg)
        banks.release(bi, tl)
        pi, pt, pg = ptd_r.next()
        te = act(pt[:, 0:W_], tS[:, 0:W_], AF.Exp, waits=[tl] + pg)
        tmpS_r.release(ti, te)
        sp_.update(pi=pi, pt=pt, te=te, nh=nh, wdt=wdt)

    def stage2(sp_):
        h, g_, d, nblk, kind, blks = sp_["h"], sp_["g"], sp_["d"], sp_["nblk"], sp_["kind"], sp_["blks"]
        vh, pt, te, nh, wdt = sp_["vh"], sp_["pt"], sp_["te"], sp_["nh"], sp_["wdt"]
        bn, bankn, gn = banks.next()
        bd, bankd, gd = banks.next()
        tok2 = None
        first_mm = True
        for b_, (r, n) in enumerate(blks):
            for hf in range(nh):
                kn = (n - 1 + hf) if kind == "full" else n
                vb = (r * nblk + kn) * 128
                mm(bankn[:, b_ * 128:(b_ + 1) * 128], vh[:, vb:vb + 128],
                   pt[:, b_ * wdt + hf * 128: b_ * wdt + hf * 128 + 128], hf == 0, hf == nh - 1,
                   waits=([te] + gn + gd + ONES) if first_mm else ())
                first_mm = False
        for b_, (r, n) in enumerate(blks):
            for hf in range(nh):
                tok2 = mm(bankd[:, b_ * 128:(b_ + 1) * 128], ones_bf[:],
                          pt[:, b_ * wdt + hf * 128: b_ * wdt + hf * 128 + 128], hf == 0, hf == nh - 1,
                          signal=(b_ == len(blks) - 1 and hf == nh - 1))
        ptd_r.release(sp_["pi"], tok2)
        ta = None
        for b_, (r, n) in enumerate(blks):
            sl = tokslice(r, n, d)
            if g_ == 0:
                ta = cp(accn[:, sl], bankn[:, b_ * 128:(b_ + 1) * 128], waits=[tok2] + dstate["acc_guard"])
                ta = cp(accd[:, sl], bankd[:, b_ * 128:(b_ + 1) * 128], waits=[ta])
            else:
                ta = tt(accn[:, sl], accn[:, sl], bankn[:, b_ * 128:(b_ + 1) * 128], ALU.add,
                        waits=[tok2, dstate["acc_last"]])
                ta = tt(accd[:, sl], accd[:, sl], bankd[:, b_ * 128:(b_ + 1) * 128], ALU.add, waits=[ta])
            dstate["acc_last"] = ta
        banks.release(bn, ta)
        banks.release(bd, ta)
        if sp_["last_gh"]:
            qkv_r.release(sp_["qi"], tok2)
        if sp_["last_head"]:
            t1 = recip(accd, accd, waits=[dstate["acc_last"]])
            oi_, ob, og = oh_r.next()
            t2 = tt(ob, accn, accd, ALU.mult, waits=[t1] + og)
            dstate["acc_guard"] = [t2]
            st = R.dma("sp", ATT[h * 128:(h + 1) * 128, :], ob, f"oh{oi_}", waits=[t2])
            oh_r.release(oi_, st)

    PIPE = 2
    for i_, sp_ in enumerate(steps_all):
        stage1(sp_)
        if i_ >= PIPE:
            stage2(steps_all[i_ - PIPE])
    for sp_ in steps_all[-PIPE:]:
        stage2(sp_)
    barrier()

    R.label = "E"
    xs["r"] = Rot(list(xst_deep_aps))
    cats = [big[:, 0:16 * T], big[:, 16 * T:32 * T]]
    cat_guard = [[], []]

    def cat_load(j):
        cv = cats[j % 2].rearrange("p (c t) -> p c t", t=T)
        R.dma("sp", cv[:, 0:8, :], ATT[:, j * T:(j + 1) * T].rearrange("(c p) t -> p c t", p=128), f"cat{j % 2}",
              waits=cat_guard[j % 2])
        return R.dma("sp", cv[:, 8:16, :], MO1[:, j * T:(j + 1) * T].rearrange("(c p) t -> p c t", p=128), f"cat{j % 2}")

    l2 = cat_load(0)
    for j in range(NT):
        c0 = j * T
        cat1 = cats[j % 2]
        l2n = cat_load(j + 1) if j + 1 < NT else None
        ev = resid_evac(x2T, x3T, c0)
        for u in range(8):
            s_, wv, wtok = wget(wsrc(b_w_out, 0, 16, u * 512, 512), 16, 512)
            tok = None
            for c in range(4):
                bi, bank, g = banks.next()
                for kk in range(16):
                    tok = mm(bank[:], wv[:, kk, c * 128:(c + 1) * 128], cat1[:, kk * T:(kk + 1) * T], kk == 0, kk == 15,
                             waits=([wtok, l2] + g) if kk == 0 else (), signal=(kk == 15))
                et = ev(u * 4 + c, bank[:], tok)
                banks.release(bi, et)
            wdone(s_, tok)
        cat_guard[j % 2] = pe_now()
        l2 = l2n
    barrier()

    mlp_phase(1, G_MLP1, x3T, (xmT, xnT), x4T)
    barrier()
    R.label = "G"
    XTS = [[hT[:].bitcast(F32), big[:].bitcast(F32)],
           [r_[:].bitcast(F32) for r_ in ring]]
    gsel = {"j": 0}

    def xch(c, n=1):
        if gsel["j"] % 2 == 0:
            return XTS[0][c // 16][:, (c % 16) * T:((c % 16) + n) * T]
        return XTS[1][c // 8][:, (c % 8) * T:((c % 8) + n) * T]

    og_bufs = [qm[:, 0:4 * T].bitcast(F32), qm[:, 4 * T:8 * T].bitcast(F32), stg[0][:], stg[1][:]]
    og_r = Rot(og_bufs)
    x_guard = [[("pe", R.cnt["pe"])], [("pe", R.cnt["pe"])]]
    R.wait("sp", [("w%d" % i_, R.dcnt["w%d" % i_]) for i_ in range(NSLOT)])

    def g_load(j):
        gsel["j"] = j
        lds_ = []
        for g4 in range(8):
            dstv = xch(g4 * 4, 4).rearrange("p (c t) -> p c t", t=T)
            ld = R.dma("sp", dstv, x4T[g4 * 512:(g4 + 1) * 512, j * T:(j + 1) * T].rearrange("(c p) t -> p c t", p=128),
                       f"xg{(j % 2) * 8 + g4}", waits=x_guard[j % 2] if g4 == 0 else [])
            lds_.append(ld)
        return lds_

    lds_next = g_load(0)
    for j in range(NT):
        c0 = j * T
        lds = lds_next
        if j + 1 < NT:
            lds_next = g_load(j + 1)
        gsel["j"] = j
        last = None
        for c2 in range(16):
            srcv = xch(c2 * 2, 2)
            if c2 == 0:
                last = act(acc[:, 0:2 * T], srcv, AF.Square, waits=[lds[0]] + ns["acc_g"])
            else:
                t1 = act(sq[:, 0:2 * T], srcv, AF.Square, waits=[lds[c2 // 2], last])
                last = tt(acc[:, 0:2 * T], acc[:, 0:2 * T], sq[:, 0:2 * T], ALU.add, waits=[t1, last])
        t2 = tt(acc1[:, 0:T], acc[:, 0:T], acc[:, T:2 * T], ALU.add, waits=[last] + ns["acc1_g"])
        ns["acc_g"] = [t2]
        rs = rms_stats_b(t2, T)
        rel = None
        for c2 in range(16):
            oi, ob, og = og_r.next()
            for cc in range(2):
                c = c2 * 2 + cc
                rel = stt(ob[:, cc * T:(cc + 1) * T], xch(c), gain_ap(G_FIN, c), rstd[:, 0:T], ALU.mult, ALU.mult,
                          waits=[rs, CST] + og)
            st = R.dma("sp", outT[c2 * 256:(c2 + 1) * 256, c0:c0 + T].rearrange("(c p) t -> p c t", p=128),
                       ob.rearrange("p (c t) -> p c t", t=T), f"og{oi}", waits=[rel])
            og_r.release(oi, st)
        x_guard[j % 2] = [rel]
        ns["rstd_g"].append(rel)
    R.wait("sp", sp_dma_toks())

    sem_keys = list(Rec.ENGS) + sorted(R.dcnt.keys())
    sems = {key: es.enter_context(nc.semaphore(f"s_{key}")) for key in sem_keys}
    block = es.enter_context(nc.Block())

    def emit(eng_name):
        def body(e):
            for item in R.streams[eng_name]:
                if item[0] == "wait":
                    e.wait_ge(sems[item[1]], item[2])
                else:
                    ins = item[1](e)
                    if item[2] is not None:
                        ins.then_inc(sems[item[2][0]], item[2][1])
        return body

    block.tensor(emit("pe"))
    block.scalar(emit("act"))
    block.vector(emit("dve"))
    block.gpsimd(emit("pool"))
    block.sync(emit("sp"))
    es.close()
    k.counts = {e: len(R.streams[e]) for e in Rec.ENGS}
    k.R = R
    k.sems = {key: (R.cnt.get(key) or R.dcnt.get(key)) for key in sem_keys}
    return nc, k


def _t5_bucket(dist):
    dist = np.asarray(dist, dtype=np.int32)
    d32 = np.maximum(dist, 1).astype(np.float32)
    large = 16 + (np.log(d32 / np.float32(16)) / np.float32(math.log(2048 / 16)) * np.float32(16)).astype(np.int32)
    large = np.minimum(large, 31)
    return np.where(dist < 16, dist, large)


def _bias_tiles(rel_bias):
    kj = np.arange(128)[:, None]
    qi = np.arange(128)[None, :]
    out = np.empty((128, 24, 2, 128), np.float32)
    for g, (win, dil) in enumerate(DIL):
        d_prev = qi + 128 - kj
        d_cur = qi - kj
        for half, delta in enumerate((d_prev, d_cur)):
            valid = (delta >= 0) & (delta <= 128)
            bucket = _t5_bucket(np.maximum(delta, 0) * dil)
            for h in range(8):
                col = rel_bias[:, g * 8 + h]
                tile = col[bucket]
                out[:, g * 8 + h, half, :] = np.where(valid, tile, np.float32(NEG))
    return np.ascontiguousarray(out.reshape(128, 24 * 256))


def _prep_shared(inp):
    f = lambda a: np.ascontiguousarray(np.asarray(a, dtype=np.float32))
    gl = [inp["a_norm"][0], inp["kv_norm"], inp["b_norm"][0], inp["mem_norm"], inp["mlp_norm"][0],
          inp["mlp_norm"][1], inp["final_norm"]]
    gains = np.stack([np.asarray(g, np.float32).reshape(NCH, 128).T for g in gl], axis=1)
    invc = np.ones((4, 16), np.float32)
    for g, w in enumerate((2, 4, 8, 16)):
        invc[g] = np.float32(1.0) / np.minimum(np.arange(16) + 1, w).astype(np.float32)
    shared = {
        "gains": f(gains.reshape(128, 7 * NCH)),
        "ascale": f(np.asarray(inp["a_scale"][0], np.float32).reshape(24, 128).T),
        "invcnt": f(np.broadcast_to(invc.reshape(1, 64), (128, 64))),
        "biasT": _bias_tiles(np.asarray(inp["rel_bias"], np.float32)),
        "a_w_in": f(inp["a_w_in"][0]),
        "a_w_pg": f(np.asarray(inp["a_w_pg"][0]).reshape(4 * 768, 768)),
        "a_w_out": f(inp["a_w_out"][0]),
        "w_kv": f(inp["w_kv"]),
        "b_w_in": f(inp["b_w_in"][0]),
        "b_w_out": f(inp["b_w_out"][0]),
        "w_mem_kv": f(np.asarray(inp["w_mem_kv"]).reshape(2 * D, 2048)),
        "mlp_w1": f(np.asarray(inp["mlp_w1"]).reshape(2 * D, DFF)),
        "mlp_w2": f(np.asarray(inp["mlp_w2"]).reshape(2 * DFF, D)),
    }
    return shared


_CACHE = {}


def kernel(**inputs):
    debug = bool(int(os.environ.get("YOCO_DEBUG", "0")))
    ncores = int(os.environ.get("YOCO_CORES", "8"))
    key = ("prog", debug)
    if key not in _CACHE:
        _CACHE[key] = build_program(debug=debug)
    nc, kinfo = _CACHE[key]
    shared = _prep_shared(inputs)
    x = np.asarray(inputs["x"], np.float32)
    mem = np.asarray(inputs["mem"], np.float32)
    in_maps = []
    for b in range(ncores):
        m = dict(shared)
        m["xT"] = np.ascontiguousarray(x[b].T)
        m["memT"] = np.ascontiguousarray(mem[b].T)
        in_maps.append(m)
    res = run_bass_kernel_spmd(nc, in_maps, core_ids=list(range(ncores)))
    if debug:
        _CACHE["last"] = res
    out = np.stack([np.ascontiguousarray(res.results[b]["outT"].T) for b in range(ncores)], axis=0)
    return out.astype(np.float32)
```

```python
import os
import math
import contextlib
import numpy as np
import concourse.bass as bass
import concourse.mybir as mybir
from concourse.bass_utils import run_bass_kernel_spmd

F32 = mybir.dt.float32
BF16 = mybir.dt.bfloat16
AF = mybir.ActivationFunctionType
ALU = mybir.AluOpType

D = 4096
S = 2048
NCH = 32
T = 512
NT = S // T
MEM = 256
DFF = 16384
EPS = 1e-6
NEG = -30000.0
NSLOT = 4
SLOT_ELEMS = 8192
DIL = ((128, 1), (512, 4), (2048, 16))

G_A, G_KV, G_B, G_MEM, G_MLP0, G_MLP1, G_FIN = range(7)


class Rec:
    ENGS = ("pe", "act", "dve", "pool", "sp")

    def __init__(self):
        self.streams = {e: [] for e in self.ENGS}
        self.cnt = {e: 0 for e in self.ENGS}
        self.seen = {e: {} for e in self.ENGS}
        self.dcnt = {}
        self.label = ""

    def _flat(self, waits, out):
        for tok in waits:
            if tok is None:
                continue
            if isinstance(tok, list):
                self._flat(tok, out)
            else:
                key, val = tok
                if val > out.get(key, 0):
                    out[key] = val
        return out

    def _waits(self, eng, waits):
        st = self.streams[eng]
        seen = self.seen[eng]
        for key, val in self._flat(waits, {}).items():
            if key == "pe" and eng == "pe":
                continue
            if seen.get(key, 0) >= val:
                continue
            seen[key] = val
            st.append(("wait", key, val, self.label))

    def op(self, eng, fn, waits=(), signal=True):
        self._waits(eng, waits)
        if signal:
            self.cnt[eng] += 1
            tok = (eng, self.cnt[eng])
            self.streams[eng].append(("op", fn, (eng, 1)))
            return tok
        self.streams[eng].append(("op", fn, None))
        return None

    def dma(self, eng, out, in_, dsem, waits=()):
        self._waits(eng, waits)
        self.dcnt[dsem] = self.dcnt.get(dsem, 0) + 16
        self.streams[eng].append(("op", lambda e, o=out, i=in_: e.dma_start(out=o, in_=i), (dsem, 16)))
        return (dsem, self.dcnt[dsem])

    def wait(self, eng, waits):
        self._waits(eng, waits)


class K:
    pass


def build_program(debug=False):
    nc = bass.Bass("TRN2", target_bir_lowering=False)
    R = Rec()
    k = K()

    def din(name, shape, dt=F32):
        return nc.dram_tensor(name, list(shape), dt, kind="ExternalInput").ap()

    def dscr(name, shape, dt):
        kind = "ExternalOutput" if debug else "Internal"
        return nc.dram_tensor(name, list(shape), dt, kind=kind).ap()

    xT = din("xT", [D, S])
    memT = din("memT", [D, MEM])
    gains = din("gains", [128, 7 * NCH])
    ascale = din("ascale", [128, 24])
    invcnt = din("invcnt", [128, 4 * 16])
    biasT = din("biasT", [128, 24 * 256])
    a_w_in = din("a_w_in", [D, D])
    a_w_pg = din("a_w_pg", [4 * 768, 768])
    a_w_out = din("a_w_out", [D, D])
    w_kv = din("w_kv", [D, 6144])
    b_w_in = din("b_w_in", [D, D])
    b_w_out = din("b_w_out", [2048, D])
    w_mem_kv = din("w_mem_kv", [2 * D, 2048])
    mlp_w1 = din("mlp_w1", [2 * D, DFF])
    mlp_w2 = din("mlp_w2", [2 * DFF, D])
    outT = nc.dram_tensor("outT", [D, S], F32, kind="ExternalOutput").ap()

    x1T = dscr("x1T", [D, S], F32)
    xmT = dscr("xmT", [D, S], F32)
    x2T = dscr("x2T", [D, S], F32)
    x3T = dscr("x3T", [D, S], F32)
    xnT = dscr("xnT", [D, S], F32)
    x4T = dscr("x4T", [D, S], F32)
    KTd = dscr("KTd", [3072, S], BF16)
    Vd = dscr("Vd", [S, 3072], BF16)
    QTd = dscr("QTd", [3072, S], BF16)
    MO1 = dscr("MO1", [1024, S], BF16)
    ATT = dscr("ATT", [1024, S], BF16)

    es = contextlib.ExitStack()

    def sb(name, shape, dt):
        return es.enter_context(nc.sbuf_tensor(name, list(shape), dt))

    ring = [sb(f"ring{i}", [128, SLOT_ELEMS], BF16) for i in range(NSLOT)]
    hT = sb("hT", [128, NCH * T], BF16)
    big = sb("big", [128, 16384], BF16)
    qm = sb("qm", [128, 8 * T], BF16)
    stg = [sb(f"stg{i}", [128, 2 * T], F32) for i in range(2)]
    sq = sb("sq", [128, 2 * T], F32)
    acc = sb("acc", [128, 2 * T], F32)
    acc1 = sb("acc1", [128, T], F32)
    rt = sb("rt", [128, T], F32)
    rstd = sb("rstd", [128, T], F32)
    mkT = sb("mkT", [128, 2 * 8 * MEM], BF16)
    mv = sb("mv", [128, 2 * 2 * 1024], BF16)
    gains_sb = sb("gains_sb", [128, 7 * NCH], F32)
    ascale_sb = sb("ascale_sb", [128, 24], F32)
    invcnt_sb = sb("invcnt_sb", [128, 64], F32)
    ones_bf = sb("ones_bf", [128, 128], BF16)
    ones_f = sb("ones_f", [128, 128], F32)
    xst = [sb(f"xst{i}", [128, T], F32) for i in range(2)]
    ost = [sb(f"ost{i}", [128, T], F32) for i in range(2)]
    bst = [sb(f"bst{i}", [128, T], BF16) for i in range(3)]
    ubuf = [sb(f"ubuf{i}", [128, 16 + T], F32) for i in range(2)]
    sA = sb("sA", [128, 16 + T], F32)
    sB = sb("sB", [128, 16 + T], F32)
    halo = sb("halo", [128, 24 * 16], F32)
    fix16 = sb("fix16", [128, 16], F32)
    ftmp = [ubuf[0][:, 0:T], ubuf[1][:, 0:T]]
    xstx = [sb(f"xstx{i}", [128, T], F32) for i in range(2)]
    pt_sb = [sb(f"pt{i}", [128, 2 * T], BF16) for i in range(2)]
    rden = [sb(f"rden{i}", [128, T], F32) for i in range(1)]

    psum = [es.enter_context(nc.psum_tensor(f"ps{i}", [128, T], F32)) for i in range(8)]

    class Rot:
        def __init__(self, bufs):
            self.bufs = bufs
            self.i = 0
            self.guard = [[] for _ in bufs]

        def next(self):
            i = self.i
            self.i = (self.i + 1) % len(self.bufs)
            g = self.guard[i]
            self.guard[i] = []
            return i, self.bufs[i], g

        def release(self, i, tok):
            self.guard[i].append(tok)

    banks = Rot(psum[:7])
    STAT = psum[7]
    ost_r = Rot(ost)
    bst_r = Rot(bst)
    ftmp_r = Rot(ftmp)
    stg_r = Rot(stg)
    ubuf_r = Rot(ubuf)
    pt_r = Rot(pt_sb)
    rden_r = Rot(rden)

    def sp_dma_toks():
        return [(key, v) for key, v in R.dcnt.items() if not key.startswith("w")]

    def barrier():
        toks = [(e, R.cnt[e]) for e in ("pe", "act", "dve") if R.cnt[e] > 0] + sp_dma_toks()
        for e in ("pe", "act", "dve", "sp"):
            R.wait(e, [t for t in toks if t[0] != e])

    wstate = {"n": 0, "free": [[] for _ in range(NSLOT)]}

    def wget(src_ap, kc, cols):
        assert kc * cols <= SLOT_ELEMS
        s_ = wstate["n"] % NSLOT
        wstate["n"] += 1
        view = ring[s_][:, 0:kc * cols].rearrange("p (k c) -> p k c", c=cols)
        g = wstate["free"][s_]
        wstate["free"][s_] = []
        tok = R.dma("pool", view, src_ap, f"w{s_}", waits=g)
        return s_, view, tok

    def wdone(s_, tok):
        wstate["free"][s_].append(tok)

    def wsrc(w, r0, kc, c0, cols):
        return w[r0:r0 + kc * 128, c0:c0 + cols].rearrange("(k p) c -> p k c", p=128)

    def mm(out, lhsT, rhs, start, stop, waits=(), signal=False):
        return R.op("pe", lambda e: e.matmul(out, lhsT, rhs, start=start, stop=stop), waits=waits, signal=signal)

    def act(out, in_, func, waits=(), scale=1.0, bias=None):
        if bias is None:
            return R.op("act", lambda e: e.activation(out=out, in_=in_, func=func, scale=scale), waits=waits)
        return R.op("act", lambda e: e.activation(out=out, in_=in_, func=func, bias=bias, scale=scale), waits=waits)

    def tt(out, in0, in1, op, waits=(), eng="dve"):
        return R.op(eng, lambda e: e.tensor_tensor(out=out, in0=in0, in1=in1, op=op), waits=waits)

    def stt(out, in0, scalar, in1, op0, op1, waits=(), eng="dve"):
        return R.op(eng, lambda e: e.scalar_tensor_tensor(out=out, in0=in0, scalar=scalar, in1=in1, op0=op0, op1=op1),
                    waits=waits)

    def cp(out, in_, waits=(), eng="dve"):
        return R.op(eng, lambda e: e.tensor_copy(out=out, in_=in_), waits=waits)

    def recip(out, in_, waits=()):
        return R.op("dve", lambda e: e.reciprocal(out=out, in_=in_), waits=waits)

    def memset(ap, val, waits=()):
        return R.op("dve", lambda e: e.memset(ap, val), waits=waits)

    R.dma("sp", gains_sb[:], gains[:], "cst")
    R.dma("sp", ascale_sb[:], ascale[:], "cst")
    CST = R.dma("sp", invcnt_sb[:], invcnt[:], "cst")
    eps_sb = sb("eps_sb", [128, 1], F32)
    ONES = [memset(ones_f[:], 1.0), memset(ones_bf[:], 1.0), memset(halo[:], 0.0), memset(eps_sb[:], EPS)]

    def gain_ap(gi, c):
        return gains_sb[:, gi * NCH + c: gi * NCH + c + 1]

    ns = {"acc_g": [], "acc1_g": [], "rt_g": [], "stat_g": [], "rstd_g": []}

    def rms_stats(src, c0, n, split=False):
        last = None
        for cg in range(NCH // 2):
            si, sbuf_, g = stg_r.next()
            sv = sbuf_[:, 0:2 * n].rearrange("p (c t) -> p c t", t=n)
            ld = R.dma("sp", sv, src[cg * 256:(cg + 1) * 256, c0:c0 + n].rearrange("(c p) t -> p c t", p=128),
                       f"stg{si}", waits=g)
            if cg == 0:
                t1 = act(acc[:, 0:2 * n], sbuf_[:, 0:2 * n], AF.Square, waits=[ld] + ns["acc_g"])
                stg_r.release(si, t1)
                last = t1
            else:
                t1 = act(sq[:, 0:2 * n], sbuf_[:, 0:2 * n], AF.Square, waits=[ld, last])
                stg_r.release(si, t1)
                last = tt(acc[:, 0:2 * n], acc[:, 0:2 * n], sq[:, 0:2 * n], ALU.add, waits=[t1, last])
        t2 = tt(acc1[:, 0:n], acc[:, 0:n], acc[:, n:2 * n], ALU.add, waits=[last] + ONES + ns["acc1_g"])
        ns["acc_g"] = [t2]
        if split:
            return t2
        return rms_stats_b(t2, n)

    def rms_stats_b(t2, n):
        t3 = mm(STAT[:, 0:n], ones_f[:], acc1[:, 0:n], True, True, waits=[t2] + ns["stat_g"], signal=True)
        ns["acc1_g"] = [t3]
        t4 = act(rt[:, 0:n], STAT[:, 0:n], AF.Sqrt, waits=[t3] + ns["rt_g"], scale=1.0 / D, bias=eps_sb[:, 0:1])
        ns["stat_g"] = [t4]
        t5 = recip(rstd[:, 0:n], rt[:, 0:n], waits=[t4] + ns["rstd_g"])
        ns["rt_g"] = [t5]
        ns["rstd_g"] = []
        return t5

    def rms_apply(src, c0, n, rstd_tok, outs, h_guard):
        tk = None
        for cg in range(NCH // 2):
            si, sbuf_, g = stg_r.next()
            sv = sbuf_[:, 0:2 * n].rearrange("p (c t) -> p c t", t=n)
            ld = R.dma("sp", sv, src[cg * 256:(cg + 1) * 256, c0:c0 + n].rearrange("(c p) t -> p c t", p=128),
                       f"stg{si}", waits=g)
            for cc in range(2):
                c = cg * 2 + cc
                for (gi, dst) in outs:
                    tk = stt(dst(c), sbuf_[:, cc * n:(cc + 1) * n], gain_ap(gi, c), rstd[:, 0:n], ALU.mult, ALU.mult,
                             waits=[ld, rstd_tok, CST] + h_guard)
            stg_r.release(si, tk)
        ns["rstd_g"].append(tk)
        return tk

    def _stg_load(src, c0, n, cg):
        si, sbuf_, g = stg_r.next()
        sv = sbuf_[:, 0:2 * n].rearrange("p (c t) -> p c t", t=n)
        ld = R.dma("sp", sv, src[cg * 256:(cg + 1) * 256, c0:c0 + n].rearrange("(c p) t -> p c t", p=128),
                   f"stg{si}", waits=g)
        return si, sbuf_, ld

    def rms_stats_gen(src, c0, n, out):
        last = None
        nxt = _stg_load(src, c0, n, 0)
        for cg in range(NCH // 2):
            si, sbuf_, ld = nxt
            if cg == 0:
                t1 = act(acc[:, 0:2 * n], sbuf_[:, 0:2 * n], AF.Square, waits=[ld] + ns["acc_g"])
                stg_r.release(si, t1)
                last = t1
            else:
                t1 = act(sq[:, 0:2 * n], sbuf_[:, 0:2 * n], AF.Square, waits=[ld, last])
                stg_r.release(si, t1)
                last = tt(acc[:, 0:2 * n], acc[:, 0:2 * n], sq[:, 0:2 * n], ALU.add, waits=[t1, last])
            if cg + 1 < NCH // 2:
                nxt = _stg_load(src, c0, n, cg + 1)
            else:
                t2 = tt(acc1[:, 0:n], acc[:, 0:n], acc[:, n:2 * n], ALU.add, waits=[last] + ONES + ns["acc1_g"])
                ns["acc_g"] = [t2]
                out["t2"] = t2
            yield

    def rms_apply_gen(src, c0, n, rstd_tok, gi, dst, h_guard, out):
        tk = None
        nxt = _stg_load(src, c0, n, 0)
        for cg in range(NCH // 2):
            si, sbuf_, ld = nxt
            if cg + 1 < NCH // 2:
                nxt = _stg_load(src, c0, n, cg + 1)
            for cc in range(2):
                c = cg * 2 + cc
                tk = stt(dst(c), sbuf_[:, cc * n:(cc + 1) * n], gain_ap(gi, c), rstd[:, 0:n], ALU.mult, ALU.mult,
                         waits=[ld, rstd_tok, CST] + h_guard)
            stg_r.release(si, tk)
            out["tok"] = tk
            yield
        ns["rstd_g"].append(tk)

    def proj_A(w, r0, nk, c0, ncols, rhs_fn, rhs_tok, n, evac, cols=512, kcu=16):
        assert kcu * cols <= SLOT_ELEMS and nk % kcu == 0 and ncols % cols == 0
        mpu = cols // 128
        nu = nk // kcu
        for gidx in range(ncols // cols):
            bb = [banks.next() for _ in range(mpu)]
            allg = [b_[2] for b_ in bb]
            for u in range(nu):
                s_, wv, wtok = wget(wsrc(w, r0 + u * kcu * 128, kcu, c0 + gidx * cols, cols), kcu, cols)
                tok = None
                for c in range(mpu):
                    bi, bank, g = bb[c]
                    for kk in range(kcu):
                        kabs = u * kcu + kk
                        tok = mm(bank[:, 0:n], wv[:, kk, c * 128:(c + 1) * 128], rhs_fn(kabs), kabs == 0, kabs == nk - 1,
                                 waits=([wtok, rhs_tok] + (allg if u == 0 else [])) if kk == 0 else (),
                                 signal=(kk == kcu - 1))
                    if u == nu - 1:
                        et = evac(gidx * mpu + c, bank[:, 0:n], tok)
                        banks.release(bi, et)
                wdone(s_, tok)

    def mem_attention(l, q_fn, q_tok, dst_fn, after):
        for h in range(4):
            pi, pt, pg = pt_r.next()
            etoks = []
            for mch in range(2):
                bi, bank, g = banks.next()
                tok = None
                for cc in range(2):
                    base = (l * 8 + h * 2 + cc) * MEM + mch * 128
                    tok = mm(bank[:], mkT[:, base:base + 128], q_fn(h * 2 + cc), cc == 0, cc == 1,
                             waits=([q_tok, MEMTOK] + g) if cc == 0 else (), signal=(cc == 1))
                et = act(pt[:, mch * T:(mch + 1) * T], bank[:], AF.Exp, waits=[tok] + pg, scale=1.0 / 16.0)
                banks.release(bi, et)
                etoks.append(et)
            bi, bank, g = banks.next()
            tok = None
            for mch in range(2):
                tok = mm(bank[:], ones_bf[:], pt[:, mch * T:(mch + 1) * T], mch == 0, mch == 1,
                         waits=(etoks + g + ONES) if mch == 0 else (), signal=(mch == 1))
            ri, rd, rg = rden_r.next()
            rtok = recip(rd[:], bank[:], waits=[tok] + rg)
            banks.release(bi, rtok)
            last = None
            for oc in range(2):
                bi, bank, g = banks.next()
                tok = None
                for mch in range(2):
                    base = (l * 2 + mch) * 1024 + h * 256 + oc * 128
                    tok = mm(bank[:], mv[:, base:base + 128], pt[:, mch * T:(mch + 1) * T], mch == 0, mch == 1,
                             waits=(etoks + g) if mch == 0 else (), signal=(mch == 1))
                dst, dg = dst_fn(h * 2 + oc)
                ot = tt(dst, bank[:], rd[:], ALU.mult, waits=[tok, rtok] + dg)
                banks.release(bi, ot)
                after(h * 2 + oc, ot)
                last = ot
            rden_r.release(ri, last)
            pt_r.release(pi, tok)

    xs = {"r": Rot([b_[:] for b_ in xst] + [b_[:] for b_ in xstx])}

    def resid_evac(src, dst, c0, nchunks=NCH):
        xr = xs["r"]
        depth = len(xr.bufs) - 1
        pending = {}

        def issue(m):
            xi, xb, xg = xr.next()
            ld = R.dma("sp", xb, src[m * 128:(m + 1) * 128, c0:c0 + T], f"xst{xi}", waits=xg)
            pending[m] = (xi, xb, ld)

        for m in range(min(depth, nchunks)):
            issue(m)

        def evac(m, ps, tok):
            xi, xb, ld = pending.pop(m)
            oi, ob, og = ost_r.next()
            et = tt(ob[:], ps, xb, ALU.add, waits=[tok, ld] + og)
            xr.release(xi, et)
            if m + depth < nchunks:
                issue(m + depth)
            st = R.dma("sp", dst[m * 128:(m + 1) * 128, c0:c0 + T], ob[:], f"ost{oi}", waits=[et])
            ost_r.release(oi, st)
            return et
        return evac

    def store_bf(dst_ap, make):
        i, b, g = bst_r.next()
        et = make(b[:], g)
        st = R.dma("sp", dst_ap, b[:], f"bst{i}", waits=[et])
        bst_r.release(i, st)
        return et

    def pe_now():
        return [("pe", R.cnt["pe"])]

    R.label = "M"
    hmem = big[:, 0:NCH * MEM]
    rs = rms_stats(memT, 0, MEM)
    hm_tok = rms_apply(memT, 0, MEM, rs, [(G_MEM, lambda c: hmem[:, c * MEM:(c + 1) * MEM])], [])
    for l in range(2):
        def ev_k(m, ps, tok, l=l):
            return act(mkT[:, (l * 8 + m) * MEM:(l * 8 + m + 1) * MEM], ps, AF.Copy, waits=[tok])
        proj_A(w_mem_kv, l * D, NCH, 0, 1024, lambda kk: hmem[:, kk * MEM:(kk + 1) * MEM], hm_tok, MEM, ev_k)
        for cg in range(2):
            bb = [banks.next() for _ in range(2)]
            tok = None
            for u in range(2):
                s_, wv, wtok = wget(wsrc(w_mem_kv, l * D + u * 2048, 16, 1024 + cg * 512, 512), 16, 512)
                for mch in range(2):
                    bi, bank, g = bb[mch]
                    for kk in range(16):
                        kabs = u * 16 + kk
                        tok = mm(bank[:], hmem[:, kabs * MEM + mch * 128: kabs * MEM + mch * 128 + 128], wv[:, kk, :],
                                 kabs == 0, kabs == NCH - 1,
                                 waits=([wtok, hm_tok] + (g if u == 0 else [])) if kk == 0 else (),
                                 signal=(kk == 15))
                wdone(s_, tok)
            for mch in range(2):
                bi, bank, g = bb[mch]
                base = (l * 2 + mch) * 1024 + cg * 512
                et = act(mv[:, base:base + 512], bank[:], AF.Copy, waits=[tok])
                banks.release(bi, et)
    MEMTOK = ("act", R.cnt["act"])
    barrier()

    pooledT = big[:, 0:24 * T]
    concat = hT
    qmem = qm
    h_guard = []
    halo_tok = {}
    t2n = rms_stats(xT, 0, T, split=True)
    for j in range(NT):
        c0 = j * T
        R.label = "A.norm"
        rs = rms_stats_b(t2n, T)
        h_tok = rms_apply(xT, c0, T, rs, [(G_A, lambda c: hT[:, c * T:(c + 1) * T])], h_guard)
        state = {"pooled_last": None, "q_last": None, "big_guard": list(h_guard), "s_guard": []}

        def ev_in(m, ps, tok, j=j, state=state):
            if m >= 24:
                t_ = act(qmem[:, (m - 24) * T:(m - 23) * T], ps, AF.Copy, waits=[tok] + state["big_guard"])
                state["q_last"] = t_
                return t_
            g_ = m // 6
            L = g_ + 1
            w_ = 2 ** L
            ui, ub, ug = ubuf_r.next()
            t_ = act(ub[:, 16:16 + T], ps, AF.Copy, waits=[tok] + ug)
            if j == 0:
                th = memset(ub[:, 0:16], 0.0, waits=ug)
            else:
                th = cp(ub[:, 0:16], halo[:, m * 16:(m + 1) * 16], waits=ug + [halo_tok[m]])
            src_ = ub
            pp = [sA, sB]
            lastt = [t_, th]
            for lv in range(1, L + 1):
                sh = 2 ** (lv - 1)
                lo = 2 ** lv - 1
                dst_ = pp[(lv - 1) % 2]
                tl = tt(dst_[:, lo:16 + T], src_[:, lo:16 + T], src_[:, lo - sh:16 + T - sh], ALU.add,
                        waits=lastt + state["s_guard"])
                lastt = [tl]
                src_ = dst_
            tf = stt(pooledT[:, m * T:(m + 1) * T], src_[:, 16:16 + T], 1.0 / w_, ub[:, 16:16 + T], ALU.mult,
                     ALU.subtract, waits=lastt + state["big_guard"])
            if j == 0:
                tf1 = tt(fix16[:, 0:16], src_[:, 16:32], invcnt_sb[:, g_ * 16:(g_ + 1) * 16], ALU.mult,
                         waits=[tf, CST])
                tf = tt(pooledT[:, m * T:m * T + 16], fix16[:, 0:16], ub[:, 16:32], ALU.subtract, waits=[tf1])
            th2 = cp(halo[:, m * 16:(m + 1) * 16], ub[:, T:T + 16], waits=[tf])
            halo_tok[m] = th2
            ubuf_r.release(ui, th2)
            state["s_guard"] = [tf]
            state["pooled_last"] = th2
            return t_

        R.label = "A.in"
        proj_A(a_w_in, 0, NCH, 0, D, lambda kk: hT[:, kk * T:(kk + 1) * T], h_tok, T, ev_in)
        cc_tok = {}
        R.label = "A.pg"
        for g_ in range(4):
            s_, wv, wtok = wget(wsrc(a_w_pg, g_ * 768, 6, 0, 768), 6, 768)
            tok = None
            for dc in range(6):
                bi, bank, bg = banks.next()
                for cc in range(6):
                    tok = mm(bank[:], wv[:, cc, dc * 128:(dc + 1) * 128],
                             pooledT[:, (g_ * 6 + cc) * T:(g_ * 6 + cc + 1) * T],
                             cc == 0, cc == 5, waits=([wtok, state["pooled_last"]] + bg) if cc == 0 else (),
                             signal=(cc == 5))
                m = g_ * 6 + dc
                et = act(concat[:, m * T:(m + 1) * T], bank[:], AF.Copy, waits=[tok, CST],
                         scale=ascale_sb[:, m:m + 1])
                banks.release(bi, et)
                cc_tok[m] = et
            wdone(s_, tok)
        R.label = "A.mem"
        mem_attention(0, lambda c: qmem[:, c * T:(c + 1) * T], state["q_last"],
                      lambda c: (concat[:, (24 + c) * T:(25 + c) * T], []),
                      lambda c, tok: cc_tok.__setitem__(24 + c, tok))
        cat_tok = [cc_tok[m] for m in range(32)]
        if j + 1 < NT:
            R.label = "A.norm"
            t2n = rms_stats(xT, c0 + T, T, split=True)
        R.label = "A.out"
        proj_A(a_w_out, 0, NCH, 0, D, lambda kk: concat[:, kk * T:(kk + 1) * T], cat_tok, T,
               resid_evac(xT, x1T, c0))
        h_guard = pe_now()
    barrier()

    xst_deep_aps = [xst[0][:], xst[1][:], sA[:, 0:T], sB[:, 0:T], xstx[0][:], xstx[1][:]]
    FFG = 32
    NQ = DFF // (FFG * 128)

    def mlp_phase(l, gi, x_in, mids, x_out):
        aT = big
        nstate = {}
        xs["r"] = Rot(list(xst_deep_aps))
        a_guard = []
        rs = rms_stats(x_in, 0, T)
        h_tok = rms_apply(x_in, 0, T, rs, [(gi, lambda c: hT[:, c * T:(c + 1) * T])], [])
        for j in range(NT):
            c0 = j * T
            h_next = None
            for q in range(NQ):
                a_last = {}
                ag = list(a_guard)

                def ev1(m, ps, tok, ag=ag, a_last=a_last):
                    fi, fb, fg = ftmp_r.next()
                    t1 = act(fb, ps, AF.Relu, waits=[tok] + fg)
                    t2 = tt(aT[:, m * T:(m + 1) * T], fb, fb, ALU.mult, waits=[t1] + ag)
                    ftmp_r.release(fi, t2)
                    a_last["t"] = t2
                    return t1

                R.label = f"mlp{l}.up"
                proj_A(mlp_w1, l * D, NCH, q * FFG * 128, FFG * 128, lambda kk: hT[:, kk * T:(kk + 1) * T], h_tok, T, ev1)
                a_tok = a_last["t"]
                R.label = f"mlp{l}.norm"
                side = None
                sout = {}
                if q == NQ - 2 and j + 1 < NT:
                    side = rms_stats_gen(x_in, c0 + T, T, nstate)
                if q == NQ - 1 and j + 1 < NT:
                    rs = rms_stats_b(nstate["t2"], T)
                    side = rms_apply_gen(x_in, c0 + T, T, rs, gi, lambda c: hT[:, c * T:(c + 1) * T], pe_now(), sout)
                src = x_in if q == 0 else mids[(q - 1) % 2]
                dst = x_out if q == NQ - 1 else mids[q % 2]
                if q > 0:
                    R.wait("sp", sp_dma_toks())
                R.label = f"mlp{l}.down"
                ev2 = resid_evac(src, dst, c0)
                for dg in range(8):
                    bb = [banks.next() for _ in range(4)]
                    allg = [b_[2] for b_ in bb]
                    for u in range(2):
                        s_, wv, wtok = wget(wsrc(mlp_w2, l * DFF + q * FFG * 128 + u * 2048, 16, dg * 512, 512), 16, 512)
                        tok = None
                        for c in range(4):
                            bi, bank, g = bb[c]
                            for kk in range(16):
                                kabs = u * 16 + kk
                                tok = mm(bank[:], wv[:, kk, c * 128:(c + 1) * 128], aT[:, kabs * T:(kabs + 1) * T],
                                         kabs == 0, kabs == FFG - 1,
                                         waits=([wtok, a_tok] + (allg if u == 0 else [])) if kk == 0 else (),
                                         signal=(kk == 15))
                            if u == 1:
                                et = ev2(dg * 4 + c, bank[:], tok)
                                banks.release(bi, et)
                                if side is not None:
                                    next(side, None)
                        wdone(s_, tok)
                if side is not None:
                    for _ in side:
                        pass
                if q == NQ - 1 and j + 1 < NT:
                    h_next = sout["tok"]
                a_guard = pe_now()
            h_tok = h_next

    mlp_phase(0, G_MLP0, x1T, (xmT, xnT), x2T)
    barrier()

    hT2 = big
    qmem1 = qm
    rs = rms_stats(x2T, 0, T)
    hk_tok = rms_apply(x2T, 0, T, rs, [(G_KV, lambda c: hT[:, c * T:(c + 1) * T])], [])
    hb_tok = rms_apply(x2T, 0, T, rs, [(G_B, lambda c: hT2[:, c * T:(c + 1) * T])], [])
    for j in range(NT):
        c0 = j * T
        R.label = "C"
        h_tok = [hk_tok, hb_tok]

        def ev_kT(m, ps, tok, c0=c0):
            return store_bf(KTd[m * 128:(m + 1) * 128, c0:c0 + T],
                            lambda b, g: act(b, ps, AF.Copy, waits=[tok] + g))

        proj_A(w_kv, 0, NCH, 0, 3072, lambda kk: hT[:, kk * T:(kk + 1) * T], hk_tok, T, ev_kT)
        for cg in range(6):
            bb = [banks.next() for _ in range(4)]
            tok = None
            for u in range(2):
                s_, wv, wtok = wget(wsrc(w_kv, u * 2048, 16, 3072 + cg * 512, 512), 16, 512)
                for tc in range(4):
                    bi, bank, g = bb[tc]
                    for kk in range(16):
                        kabs = u * 16 + kk
                        tok = mm(bank[:], hT[:, kabs * T + tc * 128: kabs * T + tc * 128 + 128], wv[:, kk, :],
                                 kabs == 0, kabs == NCH - 1,
                                 waits=([wtok, hk_tok] + (g if u == 0 else [])) if kk == 0 else (),
                                 signal=(kk == 15))
                wdone(s_, tok)
            for tc in range(4):
                bi, bank, g = bb[tc]
                et = store_bf(Vd[c0 + tc * 128:c0 + (tc + 1) * 128, cg * 512:(cg + 1) * 512],
                              lambda b, gg, bank=bank, tok=tok: act(b, bank[:], AF.Copy, waits=[tok] + gg))
                banks.release(bi, et)
        if j + 1 < NT:
            rs = rms_stats(x2T, c0 + T, T)
            hk_tok = rms_apply(x2T, c0 + T, T, rs, [(G_KV, lambda c: hT[:, c * T:(c + 1) * T])], pe_now())
        qstate = {"q_last": None}

        def ev_q(m, ps, tok, c0=c0, qstate=qstate):
            if m >= 24:
                t_ = act(qmem1[:, (m - 24) * T:(m - 23) * T], ps, AF.Copy, waits=[tok])
                qstate["q_last"] = t_
                return t_
            return store_bf(QTd[m * 128:(m + 1) * 128, c0:c0 + T],
                            lambda b, g: act(b, ps, AF.Copy, waits=[tok] + g))

        proj_A(b_w_in, 0, NCH, 0, D, lambda kk: hT2[:, kk * T:(kk + 1) * T], hb_tok, T, ev_q)
        hb_guard = pe_now()
        pend = {}

        def mo_dst(c, pend=pend):
            i, b, g = bst_r.next()
            pend[c] = (i, b)
            return b[:], g

        def mo_after(c, tok, pend=pend, c0=c0):
            i, b = pend[c]
            st = R.dma("sp", MO1[c * 128:(c + 1) * 128, c0:c0 + T], b[:], f"bst{i}", waits=[tok])
            bst_r.release(i, st)

        mem_attention(1, lambda c: qmem1[:, c * T:(c + 1) * T], qstate["q_last"], mo_dst, mo_after)
        if j + 1 < NT:
            hb_tok = rms_apply(x2T, c0 + T, T, rs, [(G_B, lambda c: hT2[:, c * T:(c + 1) * T])], hb_guard)
    barrier()

    R.label = "D"
    qkv = []
    bt_t = []
    for i in range(2):
        o = i * 6656
        qkv.append((big[:, o:o + 2048], big[:, o + 2048:o + 4096], big[:, o + 4096:o + 6144]))
        bt_t.append(big[:, o + 6144:o + 6656].bitcast(F32))
    accn = hT[:, 0:4096].bitcast(F32)
    accd = hT[:, 4096:8192].bitcast(F32)
    tmpS = [hT[:, 8192 + i * 1024: 8192 + (i + 1) * 1024].bitcast(F32) for i in range(2)]
    ptd = [hT[:, 10240 + i * 512: 10240 + (i + 1) * 512] for i in range(2)]
    oh = [hT[:, 11264 + i * 2048: 11264 + (i + 1) * 2048] for i in range(2)]
    qkv_r = Rot([(q_[0], q_[1], q_[2], bt_t[i]) for i, q_ in enumerate(qkv)])
    oh_r = Rot(oh)
    SC = 1.0 / math.sqrt(128.0)
    tmpS_r = Rot(tmpS + [sq[:, 0:T], sq[:, T:2 * T]])
    ptd_r = Rot(ptd + [qm[:, 0:T], qm[:, T:2 * T]])

    steps_all = []
    for h in range(8):
        for g_, (win, d) in enumerate(DIL):
            nblk = (S // d) // 128
            full = [(r, n) for r in range(d) for n in range(1, nblk)]
            first = [(r, 0) for r in range(d)]
            st_list = [("full", full[i:i + 2]) for i in range(0, len(full), 2)] + \
                      [("first", first[i:i + 4]) for i in range(0, len(first), 4)]
            for si_, (kind, blks) in enumerate(st_list):
                steps_all.append(dict(h=h, g=g_, d=d, nblk=nblk, kind=kind, blks=blks, first_gh=(si_ == 0),
                                      last_gh=(si_ == len(st_list) - 1), last_head=(g_ == 2 and si_ == len(st_list) - 1)))

    cur = {}
    dstate = {"acc_last": None, "acc_guard": []}

    def tokslice(r, n, d):
        st_ = r + d * 128 * n
        return slice(st_, st_ + 127 * d + 1, d)

    def stage1(sp_):
        h, g_, d, nblk, kind, blks = sp_["h"], sp_["g"], sp_["d"], sp_["nblk"], sp_["kind"], sp_["blks"]
        gh = g_ * 8 + h
        if sp_["first_gh"]:
            qi_, (qh, kh, vh, bth), qg = qkv_r.next()
            R.dma("sp", qh, QTd[gh * 128:(gh + 1) * 128, :], f"qkv{qi_}", waits=qg)
            R.dma("sp", kh, KTd[gh * 128:(gh + 1) * 128, :], f"qkv{qi_}")
            ld = R.dma("sp", bth, biasT[:, gh * 256:(gh + 1) * 256], f"qkv{qi_}")
            vsrc = Vd[:, gh * 128:(gh + 1) * 128].rearrange("(n i r) c -> i r n c", i=128, r=d)
            vdst = vh.rearrange("p (r n c) -> p r n c", r=d, n=nblk)
            for r in range(d):
                ld = R.dma("sp", vdst[:, r], vsrc[:, r], f"qkv{qi_}")
            cur.update(qi=qi_, qh=qh, kh=kh, vh=vh, bth=bth, ld=ld)
        sp_.update(qi=cur["qi"], qh=cur["qh"], kh=cur["kh"], vh=cur["vh"], bth=cur["bth"], ld=cur["ld"])
        qh, kh, bth, ld = sp_["qh"], sp_["kh"], sp_["bth"], sp_["ld"]
        nh = 2 if kind == "full" else 1
        wdt = nh * 128
        bi, bank, bg = banks.next()
        tok = None
        first_mm = True
        for b_, (r, n) in enumerate(blks):
            for hf in range(nh):
                kn = (n - 1 + hf) if kind == "full" else n
                tok = mm(bank[:, b_ * wdt + hf * 128: b_ * wdt + hf * 128 + 128],
                         kh[:, tokslice(r, kn, d)], qh[:, tokslice(r, n, d)], True, True,
                         waits=([ld] + bg) if first_mm else (),
                         signal=(b_ == len(blks) - 1 and hf == nh - 1))
                first_mm = False
        W_ = len(blks) * wdt
        ti, tS, tg = tmpS_r.next()
        tl = None
        boff = 0 if kind == "full" else 128
        for b_ in range(len(blks)):
            tl = stt(tS[:, b_ * wdt:(b_ + 1) * wdt], bank[:, b_ * wdt:(b_ + 1) * wdt], SC,
                     bth[:, boff:boff + wdt], ALU.mult, ALU.add, waits=[tok, ld] + tg)
        banks.release(bi, tl)
        pi, pt, pg = ptd_r.next()
        te = act(pt[:, 0:W_], tS[:, 0:W_], AF.Exp, waits=[tl] + pg)
        tmpS_r.release(ti, te)
        sp_.update(pi=pi, pt=pt, te=te, nh=nh, wdt=wdt)

    def stage2(sp_):
        h, g_, d, nblk, kind, blks = sp_["h"], sp_["g"], sp_["d"], sp_["nblk"], sp_["kind"], sp_["blks"]
        vh, pt, te, nh, wdt = sp_["vh"], sp_["pt"], sp_["te"], sp_["nh"], sp_["wdt"]
        bn, bankn, gn = banks.next()
        bd, bankd, gd = banks.next()
        tok2 = None
        first_mm = True
        for b_, (r, n) in enumerate(blks):
            for hf in range(nh):
                kn = (n - 1 + hf) if kind == "full" else n
                vb = (r * nblk + kn) * 128
                mm(bankn[:, b_ * 128:(b_ + 1) * 128], vh[:, vb:vb + 128],
                   pt[:, b_ * wdt + hf * 128: b_ * wdt + hf * 128 + 128], hf == 0, hf == nh - 1,
                   waits=([te] + gn + gd + ONES) if first_mm else ())
                first_mm = False
        for b_, (r, n) in enumerate(blks):
            for hf in range(nh):
                tok2 = mm(bankd[:, b_ * 128:(b_ + 1) * 128], ones_bf[:],
                          pt[:, b_ * wdt + hf * 128: b_ * wdt + hf * 128 + 128], hf == 0, hf == nh - 1,
                          signal=(b_ == len(blks) - 1 and hf == nh - 1))
        ptd_r.release(sp_["pi"], tok2)
        ta = None
        for b_, (r, n) in enumerate(blks):
            sl = tokslice(r, n, d)
            if g_ == 0:
                ta = cp(accn[:, sl], bankn[:, b_ * 128:(b_ + 1) * 128], waits=[tok2] + dstate["acc_guard"])
                ta = cp(accd[:, sl], bankd[:, b_ * 128:(b_ + 1) * 128], waits=[ta])
            else:
                ta = tt(accn[:, sl], accn[:, sl], bankn[:, b_ * 128:(b_ + 1) * 128], ALU.add,
                        waits=[tok2, dstate["acc_last"]])
                ta = tt(accd[:, sl], accd[:, sl], bankd[:, b_ * 128:(b_ + 1) * 128], ALU.add, waits=[ta])
            dstate["acc_last"] = ta
        banks.release(bn, ta)
        banks.release(bd, ta)
        if sp_["last_gh"]:
            qkv_r.release(sp_["qi"], tok2)
        if sp_["last_head"]:
            t1 = recip(accd, accd, waits=[dstate["acc_last"]])
            oi_, ob, og = oh_r.next()
            t2 = tt(ob, accn, accd, ALU.mult, waits=[t1] + og)
            dstate["acc_guard"] = [t2]
            st = R.dma("sp", ATT[h * 128:(h + 1) * 128, :], ob, f"oh{oi_}", waits=[t2])
            oh_r.release(oi_, st)

    PIPE = 2
    for i_, sp_ in enumerate(steps_all):
        stage1(sp_)
        if i_ >= PIPE:
            stage2(steps_all[i_ - PIPE])
    for sp_ in steps_all[-PIPE:]:
        stage2(sp_)
    barrier()

    R.label = "E"
    xs["r"] = Rot(list(xst_deep_aps))
    cats = [big[:, 0:16 * T], big[:, 16 * T:32 * T]]
    cat_guard = [[], []]

    def cat_load(j):
        cv = cats[j % 2].rearrange("p (c t) -> p c t", t=T)
        R.dma("sp", cv[:, 0:8, :], ATT[:, j * T:(j + 1) * T].rearrange("(c p) t -> p c t", p=128), f"cat{j % 2}",
              waits=cat_guard[j % 2])
        return R.dma("sp", cv[:, 8:16, :], MO1[:, j * T:(j + 1) * T].rearrange("(c p) t -> p c t", p=128), f"cat{j % 2}")

    l2 = cat_load(0)
    for j in range(NT):
        c0 = j * T
        cat1 = cats[j % 2]
        l2n = cat_load(j + 1) if j + 1 < NT else None
        ev = resid_evac(x2T, x3T, c0)
        for u in range(8):
            s_, wv, wtok = wget(wsrc(b_w_out, 0, 16, u * 512, 512), 16, 512)
            tok = None
            for c in range(4):
                bi, bank, g = banks.next()
                for kk in range(16):
                    tok = mm(bank[:], wv[:, kk, c * 128:(c + 1) * 128], cat1[:, kk * T:(kk + 1) * T], kk == 0, kk == 15,
                             waits=([wtok, l2] + g) if kk == 0 else (), signal=(kk == 15))
                et = ev(u * 4 + c, bank[:], tok)
                banks.release(bi, et)
            wdone(s_, tok)
        cat_guard[j % 2] = pe_now()
        l2 = l2n
    barrier()

    mlp_phase(1, G_MLP1, x3T, (xmT, xnT), x4T)
    barrier()
    R.label = "G"
    XTS = [[hT[:].bitcast(F32), big[:].bitcast(F32)],
           [r_[:].bitcast(F32) for r_ in ring]]
    gsel = {"j": 0}

    def xch(c, n=1):
        if gsel["j"] % 2 == 0:
            return XTS[0][c // 16][:, (c % 16) * T:((c % 16) + n) * T]
        return XTS[1][c // 8][:, (c % 8) * T:((c % 8) + n) * T]

    og_bufs = [qm[:, 0:4 * T].bitcast(F32), qm[:, 4 * T:8 * T].bitcast(F32), stg[0][:], stg[1][:]]
    og_r = Rot(og_bufs)
    x_guard = [[("pe", R.cnt["pe"])], [("pe", R.cnt["pe"])]]
    R.wait("sp", [("w%d" % i_, R.dcnt["w%d" % i_]) for i_ in range(NSLOT)])

    def g_load(j):
        gsel["j"] = j
        lds_ = []
        for g4 in range(8):
            dstv = xch(g4 * 4, 4).rearrange("p (c t) -> p c t", t=T)
            ld = R.dma("sp", dstv, x4T[g4 * 512:(g4 + 1) * 512, j * T:(j + 1) * T].rearrange("(c p) t -> p c t", p=128),
                       f"xg{(j % 2) * 8 + g4}", waits=x_guard[j % 2] if g4 == 0 else [])
            lds_.append(ld)
        return lds_

    lds_next = g_load(0)
    for j in range(NT):
        c0 = j * T
        lds = lds_next
        if j + 1 < NT:
            lds_next = g_load(j + 1)
        gsel["j"] = j
        last = None
        for c2 in range(16):
            srcv = xch(c2 * 2, 2)
            if c2 == 0:
                last = act(acc[:, 0:2 * T], srcv, AF.Square, waits=[lds[0]] + ns["acc_g"])
            else:
                t1 = act(sq[:, 0:2 * T], srcv, AF.Square, waits=[lds[c2 // 2], last])
                last = tt(acc[:, 0:2 * T], acc[:, 0:2 * T], sq[:, 0:2 * T], ALU.add, waits=[t1, last])
        t2 = tt(acc1[:, 0:T], acc[:, 0:T], acc[:, T:2 * T], ALU.add, waits=[last] + ns["acc1_g"])
        ns["acc_g"] = [t2]
        rs = rms_stats_b(t2, T)
        rel = None
        for c2 in range(16):
            oi, ob, og = og_r.next()
            for cc in range(2):
                c = c2 * 2 + cc
                rel = stt(ob[:, cc * T:(cc + 1) * T], xch(c), gain_ap(G_FIN, c), rstd[:, 0:T], ALU.mult, ALU.mult,
                          waits=[rs, CST] + og)
            st = R.dma("sp", outT[c2 * 256:(c2 + 1) * 256, c0:c0 + T].rearrange("(c p) t -> p c t", p=128),
                       ob.rearrange("p (c t) -> p c t", t=T), f"og{oi}", waits=[rel])
            og_r.release(oi, st)
        x_guard[j % 2] = [rel]
        ns["rstd_g"].append(rel)
    R.wait("sp", sp_dma_toks())

    sem_keys = list(Rec.ENGS) + sorted(R.dcnt.keys())
    sems = {key: es.enter_context(nc.semaphore(f"s_{key}")) for key in sem_keys}
    block = es.enter_context(nc.Block())

    def emit(eng_name):
        def body(e):
            for item in R.streams[eng_name]:
                if item[0] == "wait":
                    e.wait_ge(sems[item[1]], item[2])
                else:
                    ins = item[1](e)
                    if item[2] is not None:
                        ins.then_inc(sems[item[2][0]], item[2][1])
        return body

    block.tensor(emit("pe"))
    block.scalar(emit("act"))
    block.vector(emit("dve"))
    block.gpsimd(emit("pool"))
    block.sync(emit("sp"))
    es.close()
    k.counts = {e: len(R.streams[e]) for e in Rec.ENGS}
    k.R = R
    k.sems = {key: (R.cnt.get(key) or R.dcnt.get(key)) for key in sem_keys}
    return nc, k


def _t5_bucket(dist):
    dist = np.asarray(dist, dtype=np.int32)
    d32 = np.maximum(dist, 1).astype(np.float32)
    large = 16 + (np.log(d32 / np.float32(16)) / np.float32(math.log(2048 / 16)) * np.float32(16)).astype(np.int32)
    large = np.minimum(large, 31)
    return np.where(dist < 16, dist, large)


def _bias_tiles(rel_bias):
    kj = np.arange(128)[:, None]
    qi = np.arange(128)[None, :]
    out = np.empty((128, 24, 2, 128), np.float32)
    for g, (win, dil) in enumerate(DIL):
        d_prev = qi + 128 - kj
        d_cur = qi - kj
        for half, delta in enumerate((d_prev, d_cur)):
            valid = (delta >= 0) & (delta <= 128)
            bucket = _t5_bucket(np.maximum(delta, 0) * dil)
            for h in range(8):
                col = rel_bias[:, g * 8 + h]
                tile = col[bucket]
                out[:, g * 8 + h, half, :] = np.where(valid, tile, np.float32(NEG))
    return np.ascontiguousarray(out.reshape(128, 24 * 256))


def _prep_shared(inp):
    f = lambda a: np.ascontiguousarray(np.asarray(a, dtype=np.float32))
    gl = [inp["a_norm"][0], inp["kv_norm"], inp["b_norm"][0], inp["mem_norm"], inp["mlp_norm"][0],
          inp["mlp_norm"][1], inp["final_norm"]]
    gains = np.stack([np.asarray(g, np.float32).reshape(NCH, 128).T for g in gl], axis=1)
    invc = np.ones((4, 16), np.float32)
    for g, w in enumerate((2, 4, 8, 16)):
        invc[g] = np.float32(1.0) / np.minimum(np.arange(16) + 1, w).astype(np.float32)
    shared = {
        "gains": f(gains.reshape(128, 7 * NCH)),
        "ascale": f(np.asarray(inp["a_scale"][0], np.float32).reshape(24, 128).T),
        "invcnt": f(np.broadcast_to(invc.reshape(1, 64), (128, 64))),
        "biasT": _bias_tiles(np.asarray(inp["rel_bias"], np.float32)),
        "a_w_in": f(inp["a_w_in"][0]),
        "a_w_pg": f(np.asarray(inp["a_w_pg"][0]).reshape(4 * 768, 768)),
        "a_w_out": f(inp["a_w_out"][0]),
        "w_kv": f(inp["w_kv"]),
        "b_w_in": f(inp["b_w_in"][0]),
        "b_w_out": f(inp["b_w_out"][0]),
        "w_mem_kv": f(np.asarray(inp["w_mem_kv"]).reshape(2 * D, 2048)),
        "mlp_w1": f(np.asarray(inp["mlp_w1"]).reshape(2 * D, DFF)),
        "mlp_w2": f(np.asarray(inp["mlp_w2"]).reshape(2 * DFF, D)),
    }
    return shared


_CACHE = {}


def kernel(**inputs):
    debug = bool(int(os.environ.get("YOCO_DEBUG", "0")))
    ncores = int(os.environ.get("YOCO_CORES", "8"))
    key = ("prog", debug)
    if key not in _CACHE:
        _CACHE[key] = build_program(debug=debug)
    nc, kinfo = _CACHE[key]
    shared = _prep_shared(inputs)
    x = np.asarray(inputs["x"], np.float32)
    mem = np.asarray(inputs["mem"], np.float32)
    in_maps = []
    for b in range(ncores):
        m = dict(shared)
        m["xT"] = np.ascontiguousarray(x[b].T)
        m["memT"] = np.ascontiguousarray(mem[b].T)
        in_maps.append(m)
    res = run_bass_kernel_spmd(nc, in_maps, core_ids=list(range(ncores)))
    if debug:
        _CACHE["last"] = res
    out = np.stack([np.ascontiguousarray(res.results[b]["outT"].T) for b in range(ncores)], axis=0)
    return out.astype(np.float32)
```
